# Optimizing a Trainium2 kernel written in Bass

```python
import math
import jax, jax.numpy as jnp
from jax import lax
import numpy as np

D_MODEL = 1024
BATCH = 4
SEQ = 4096
DEPTH = 2

D_MIX = D_MODEL
S5_WIDTH = D_MIX // 2
S5_GROUP = 16
S5_GROUPS = S5_WIDTH // S5_GROUP
S5_STATE = 64
DT_MIN = 0.001
DT_MAX = 0.1
SB_WIDTH = D_MIX - S5_WIDTH
SB_HEAD_DIM = 64
SB_HEADS = SB_WIDTH // SB_HEAD_DIM
SB_BLOCK = 128
CONV_CH = D_MODEL
CONV_K = 3
D_FF = 2752
N_EVEN = (DEPTH + 1) // 2
N_ODD = DEPTH // 2
EPS = 1e-6

kernel_name = "hybrid_s5_stickbreak_shortconv_macaron"


def rmsnorm(x, g):
    xf = x.astype(jnp.float32)
    r = lax.rsqrt(jnp.mean(xf * xf, axis=-1, keepdims=True) + EPS)
    return (xf * r * g.astype(jnp.float32)).astype(x.dtype)


def swiglu(h, w_gate, w_up, w_down):
    return (jax.nn.silu(h @ w_gate) * (h @ w_up)) @ w_down


def _complex_linear_combine(e1, e2):
    a1r, a1i, b1r, b1i = e1
    a2r, a2i, b2r, b2i = e2
    return (a2r * a1r - a2i * a1i,
            a2r * a1i + a2i * a1r,
            a2r * b1r - a2i * b1i + b2r,
            a2r * b1i + a2i * b1r + b2i)


def s5_mixer(u, lam_re, lam_im, log_dt, b_re, b_im, c_re, c_im, d, w_glu):
    bsz, seq, _ = u.shape
    uf = u.astype(jnp.float32).reshape(bsz, seq, S5_GROUPS, S5_GROUP)
    lr = lam_re.astype(jnp.float32)
    li = lam_im.astype(jnp.float32)
    dt = jnp.exp(log_dt.astype(jnp.float32))[:, None]
    mag = jnp.exp(lr * dt)
    ab_re = mag * jnp.cos(li * dt)
    ab_im = mag * jnp.sin(li * dt)
    den = lr * lr + li * li
    nr = ab_re - 1.0
    coef_re = (nr * lr + ab_im * li) / den
    coef_im = (ab_im * lr - nr * li) / den
    br = b_re.astype(jnp.float32)
    bi = b_im.astype(jnp.float32)
    bb_re = coef_re[..., None] * br - coef_im[..., None] * bi
    bb_im = coef_re[..., None] * bi + coef_im[..., None] * br
    bu_re = jnp.einsum('blgp,gnp->blgn', uf, bb_re)
    bu_im = jnp.einsum('blgp,gnp->blgn', uf, bb_im)
    a_re = jnp.broadcast_to(ab_re, (seq,) + ab_re.shape)[None]
    a_im = jnp.broadcast_to(ab_im, (seq,) + ab_im.shape)[None]
    _, _, h_re, h_im = lax.associative_scan(
        _complex_linear_combine, (a_re, a_im, bu_re, bu_im), axis=1)
    y = (jnp.einsum('blgn,gpn->blgp', h_re, c_re.astype(jnp.float32))
         - jnp.einsum('blgn,gpn->blgp', h_im, c_im.astype(jnp.float32))
         + d.astype(jnp.float32).reshape(S5_GROUPS, S5_GROUP) * uf)
    y = y.reshape(bsz, seq, S5_WIDTH)
    z = jax.nn.gelu(y)
    out = z * jax.nn.sigmoid(z @ w_glu.astype(jnp.float32))
    return out.astype(u.dtype)


def stick_breaking_attention(q, k, v):
    bsz, seq, nh, dh = q.shape
    qh = q.transpose(0, 2, 1, 3).astype(jnp.float32)
    kh = k.transpose(0, 2, 1, 3).astype(jnp.float32)
    vh = v.transpose(0, 2, 1, 3)
    scale = 1.0 / math.sqrt(dh)
    key_pos = jnp.arange(seq)
    n_blocks = seq // SB_BLOCK

    def block(i):
        start = i * SB_BLOCK
        qb = lax.dynamic_slice_in_dim(qh, start, SB_BLOCK, axis=2)
        z = jnp.einsum('bhqd,bhkd->bhqk', qb, kh) * scale
        q_pos = start + jnp.arange(SB_BLOCK)
        mask = key_pos[None, :] < q_pos[:, None]
        log_keep = jnp.where(mask, jax.nn.log_sigmoid(-z), 0.0)
        later = lax.cumsum(log_keep, axis=3, reverse=True) - log_keep
        w = jnp.where(mask, jnp.exp(jax.nn.log_sigmoid(z) + later), 0.0)
        return jnp.einsum('bhqk,bhkd->bhqd', w.astype(vh.dtype), vh)

    out = lax.map(block, jnp.arange(n_blocks))
    out = out.transpose(1, 0, 3, 2, 4).reshape(bsz, seq, nh * dh)
    return out


def parallel_s5_stickbreak(h, w_in, lam_re, lam_im, log_dt, b_re, b_im, c_re, c_im, d, w_glu, w_out):
    bsz, seq, _ = h.shape
    proj = h @ w_in
    u = proj[..., :S5_WIDTH]
    qkv = proj[..., S5_WIDTH:].reshape(bsz, seq, 3, SB_HEADS, SB_HEAD_DIM)
    y_a = s5_mixer(u, lam_re, lam_im, log_dt, b_re, b_im, c_re, c_im, d, w_glu)
    y_b = stick_breaking_attention(qkv[:, :, 0], qkv[:, :, 1], qkv[:, :, 2]).astype(y_a.dtype)
    return jnp.concatenate([y_a, y_b], axis=-1) @ w_out


def short_conv_mixer(h, w_in, conv_w, w_out):
    proj = h @ w_in
    b_gate, c_gate, v = jnp.split(proj, 3, axis=-1)
    y = lax.conv_general_dilated(
        c_gate * v, conv_w[:, None, :], window_strides=(1,),
        padding=[(CONV_K - 1, 0)], dimension_numbers=('NWC', 'WIO', 'NWC'),
        feature_group_count=CONV_CH)
    return (b_gate * y) @ w_out


def setup_inputs(seed: int = 0) -> dict:
    key = jax.random.key(seed)
    ks = jax.random.split(key, 32)
    nrm = jax.random.normal
    f32 = jnp.float32
    D, F = D_MODEL, D_FF
    G, N, P = S5_GROUPS, S5_STATE, S5_GROUP
    inp = {}
    inp['x'] = nrm(ks[0], (BATCH, SEQ, D), f32)
    inp['ffn1_norm'] = 1.0 + 0.02 * nrm(ks[1], (DEPTH, D), f32)
    inp['ffn1_w_gate'] = nrm(ks[2], (DEPTH, D, F), f32) * D ** -0.5
    inp['ffn1_w_up'] = nrm(ks[3], (DEPTH, D, F), f32) * D ** -0.5
    inp['ffn1_w_down'] = nrm(ks[4], (DEPTH, F, D), f32) * F ** -0.5
    inp['mix_norm'] = 1.0 + 0.02 * nrm(ks[5], (DEPTH, D), f32)
    inp['ffn2_norm'] = 1.0 + 0.02 * nrm(ks[6], (DEPTH, D), f32)
    inp['ffn2_w_gate'] = nrm(ks[7], (DEPTH, D, F), f32) * D ** -0.5
    inp['ffn2_w_up'] = nrm(ks[8], (DEPTH, D, F), f32) * D ** -0.5
    inp['ffn2_w_down'] = nrm(ks[9], (DEPTH, F, D), f32) * F ** -0.5
    inp['ab_w_in'] = nrm(ks[10], (N_EVEN, D, S5_WIDTH + 3 * SB_WIDTH), f32) * D ** -0.5
    inp['s5_lambda_re'] = -0.5 + 0.01 * nrm(ks[11], (N_EVEN, G, N), f32)
    inp['s5_lambda_im'] = jnp.broadcast_to(math.pi * jnp.arange(N, dtype=f32), (N_EVEN, G, N)) \
        + 0.01 * nrm(ks[12], (N_EVEN, G, N), f32)
    inp['s5_log_dt'] = jax.random.uniform(ks[13], (N_EVEN, G), f32, math.log(DT_MIN), math.log(DT_MAX))
    inp['s5_b_re'] = nrm(ks[14], (N_EVEN, G, N, P), f32) * (2 * P) ** -0.5
    inp['s5_b_im'] = nrm(ks[15], (N_EVEN, G, N, P), f32) * (2 * P) ** -0.5
    inp['s5_c_re'] = nrm(ks[16], (N_EVEN, G, P, N), f32) * N ** -0.5
    inp['s5_c_im'] = nrm(ks[17], (N_EVEN, G, P, N), f32) * N ** -0.5
    inp['s5_d'] = nrm(ks[18], (N_EVEN, S5_WIDTH), f32)
    inp['s5_w_glu'] = nrm(ks[19], (N_EVEN, S5_WIDTH, S5_WIDTH), f32) * S5_WIDTH ** -0.5
    inp['ab_w_out'] = nrm(ks[20], (N_EVEN, D_MIX, D), f32) * D_MIX ** -0.5
    inp['sc_w_in'] = nrm(ks[21], (N_ODD, D, 3 * CONV_CH), f32) * D ** -0.5
    inp['sc_conv_w'] = nrm(ks[22], (N_ODD, CONV_K, CONV_CH), f32) * CONV_K ** -0.5
    inp['sc_w_out'] = nrm(ks[23], (N_ODD, CONV_CH, D), f32) * CONV_CH ** -0.5
    inp['final_norm'] = 1.0 + 0.02 * nrm(ks[24], (D,), f32)
    return inp


def reference(x, ffn1_norm, ffn1_w_gate, ffn1_w_up, ffn1_w_down, mix_norm,
              ffn2_norm, ffn2_w_gate, ffn2_w_up, ffn2_w_down,
              ab_w_in, s5_lambda_re, s5_lambda_im, s5_log_dt, s5_b_re, s5_b_im,
              s5_c_re, s5_c_im, s5_d, s5_w_glu, ab_w_out,
              sc_w_in, sc_conv_w, sc_w_out, final_norm):
    for layer in range(DEPTH):
        x = x + 0.5 * swiglu(rmsnorm(x, ffn1_norm[layer]),
                             ffn1_w_gate[layer], ffn1_w_up[layer], ffn1_w_down[layer])
        h = rmsnorm(x, mix_norm[layer])
        if layer % 2 == 0:
            e = layer // 2
            x = x + parallel_s5_stickbreak(
                h, ab_w_in[e], s5_lambda_re[e], s5_lambda_im[e], s5_log_dt[e],
                s5_b_re[e], s5_b_im[e], s5_c_re[e], s5_c_im[e], s5_d[e], s5_w_glu[e], ab_w_out[e])
        else:
            o = layer // 2
            x = x + short_conv_mixer(h, sc_w_in[o], sc_conv_w[o], sc_w_out[o])
        x = x + 0.5 * swiglu(rmsnorm(x, ffn2_norm[layer]),
                             ffn2_w_gate[layer], ffn2_w_up[layer], ffn2_w_down[layer])
    return rmsnorm(x, final_norm)
```

```python
import os
import math
import numpy as np
from contextlib import ExitStack
import concourse.bass as bass
import concourse.mybir as mybir
from concourse.bass_utils import run_bass_kernel_spmd

F32 = mybir.dt.float32
BF16 = mybir.dt.bfloat16
ALU = mybir.AluOpType
AF = mybir.ActivationFunctionType

D = 1024
NTOK = 2048
TB = 512
NTB = NTOK // TB
DFF = 2752
SEQ = 4096
PAIRS = [[0, 1], [2, 3], [4, 5], [6, 7]]
EPS = 1e-6
PI = math.pi
STAGES = ["ffn1_0", "proj0", "mixB", "mixC", "ffn2_0", "ffn1_1", "mix1", "ffn2_1", "final"]


class Ctr:
    def __init__(self, sem, name):
        self.sem, self.val, self.name = sem, 0, name


class Buf:
    def __init__(self, name="", excl=False):
        self.name, self.excl = name, excl
        self.lw = None
        self.rd = {}


class Prog:
    ENG = ("pe", "act", "dve", "pool", "sp")

    def __init__(self, nc, es):
        self.nc, self.es = nc, es
        self.streams = {e: [] for e in self.ENG}
        self.ctr = {e: Ctr(es.enter_context(nc.semaphore("c_" + e)), e) for e in ("pe", "act", "dve", "pool")}
        self.known = {e: {} for e in self.ENG}
        self.dma_ctrs = []
        self.dma_set = set()

    def dma_ctr(self, name):
        c = Ctr(self.es.enter_context(self.nc.semaphore("d_" + name)), name)
        self.dma_ctrs.append(c)
        self.dma_set.add(c)
        return c

    def _deps(self, eng, reads, writes):
        need = {}

        def add(cv):
            if cv is not None and need.get(cv[0], 0) < cv[1]:
                need[cv[0]] = cv[1]
        for r in reads:
            add(r.lw)
        for w in writes:
            add(w.lw)
            for c, v in w.rd.items():
                add((c, v))
        own = self.ctr.get(eng)
        for c, v in need.items():
            if c is own and eng == "pe":
                continue
            if c in self.dma_set:
                v = c.val
            if self.known[eng].get(c, 0) >= v:
                continue
            self.known[eng][c] = v
            self.streams[eng].append(("wait", c.sem, v))

    def _done(self, c, reads, writes):
        for w in writes:
            w.lw = (c, c.val)
            w.rd = {}
        for r in reads:
            if r.rd.get(c, 0) < c.val:
                r.rd[c] = c.val

    @staticmethod
    def _split(reads, writes):
        w = list(writes) + [r for r in reads if r.excl]
        ws = set(id(x) for x in w)
        r = [x for x in reads if id(x) not in ws]
        seen, w2 = set(), []
        for x in w:
            if id(x) not in seen:
                seen.add(id(x))
                w2.append(x)
        return r, w2

    def op(self, eng, fn, reads=(), writes=(), inc=True):
        reads, writes = self._split(reads, writes)
        self._deps(eng, reads, writes)
        c = self.ctr[eng]
        if inc:
            c.val += 1
            self.streams[eng].append(("op", fn, c.sem, 1))
            self._done(c, reads, writes)
        else:
            assert eng == "pe"
            self.streams[eng].append(("opn", fn))
            c.val += 1
            self._done(c, reads, writes)
            c.val -= 1

    def dma(self, q, ctr, fn, reads=(), writes=()):
        reads, writes = self._split(reads, writes)
        self._deps(q, reads, writes)
        ctr.val += 16
        self.streams[q].append(("op", fn, ctr.sem, 16))
        self._done(ctr, reads, writes)

    def cc(self, ctr, fn, reads=(), writes=()):
        reads, writes = self._split(reads, writes)
        self._deps("pool", reads, writes)
        ctr.val += 1
        self.streams["pool"].append(("op", fn, ctr.sem, 1))
        self._done(ctr, reads, writes)

    def barrier(self):
        allc = list(self.ctr.values()) + self.dma_ctrs
        for e in self.ENG:
            for c in allc:
                if c.val > self.known[e].get(c, 0):
                    self.known[e][c] = c.val
                    self.streams[e].append(("wait", c.sem, c.val))

    def emit(self):
        nc = self.nc
        _PAR.clear()
        with nc.Block() as block:
            def run(name):
                def f(e):
                    for it in self.streams[name]:
                        if it[0] == "wait":
                            e.wait_ge(it[1], it[2])
                        elif it[0] == "opn":
                            it[1](e)
                        else:
                            it[1](e).then_inc(it[2], it[3])
                return f
            block.tensor(run("pe"))
            block.scalar(run("act"))
            block.vector(run("dve"))
            block.gpsimd(run("pool"))
            block.sync(run("sp"))


_PAR = {}


def PART(e, s_):
    k = id(e)
    if k not in _PAR:
        par = e.partition_id() % 2
        _PAR[k] = (par, 1 - par)
    return _PAR[k][s_]


def build(stop_after="final", dbg=False):
    nc = bass.Bass("TRN2", target_bir_lowering=False)
    es = ExitStack()
    with es:
        def din(name, shape, dt=F32):
            return nc.dram_tensor(name, list(shape), dt, kind="ExternalInput")
        xT_d = din("xT", [D, NTOK])
        norms_d = din("norms", [128, 7 * 8])
        wg_d = [din(f"wg{i}", [D, DFF]) for i in range(4)]
        wu_d = [din(f"wu{i}", [D, DFF]) for i in range(4)]
        wd_d = [din(f"wd{i}", [DFF, D]) for i in range(4)]
        win_d = din("win", [D, 2048])
        wglu_d = din("wglu", [512, 512])
        wout_d = din("wout", [1024, D])
        scin_d = din("scin", [D, 3072])
        scw_d = din("scw", [128, 24])
        scout_d = din("scout", [D, D])
        lam_d = din("lam", [128, 40])
        bre_d = din("bre", [16, 64, 16])
        bim_d = din("bim", [16, 64, 16])
        cre_d = din("cre", [16, 16, 64])
        cim_d = din("cim", [16, 16, 64])
        s5d_d = din("s5d", [128, 2])
        cst_d = din("cst", [128, 128 + 128 + 4 * 512 + 512])
        flag_d = din("flag", [128, 1])
        out_d = nc.dram_tensor("outT", [D, NTOK], F32, kind="ExternalOutput")
        dbg_d = nc.dram_tensor("dbg2", [1024, 4096], BF16, kind="ExternalOutput") if stop_after == "mixB" else None
        s1p = [nc.dram_tensor(f"s1_{i}", [512, 2048], BF16) for i in range(4)]
        o1p = [nc.dram_tensor(f"o1_{i}", [1024, 2048], BF16) for i in range(4)]
        s2p = [nc.dram_tensor(f"s2_{i}", [256, 4096], BF16) for i in range(2)]
        o2p = [nc.dram_tensor(f"o2_{i}", [512, 4096], BF16) for i in range(2)]
        s3 = nc.dram_tensor("s3", [128, 16], BF16)
        o3 = nc.dram_tensor("o3", [256, 16], BF16)

        pg = Prog(nc, es)
        DBG = os.environ.get("MK_DBG", "") == "1"

        def dump_x(name):
            if not DBG:
                return
            dd = nc.dram_tensor("dbgx_" + name, [D, NTOK], F32, kind="ExternalOutput")
            dc_ = pg.dma_ctr("dbgx_" + name)
            for c in range(8):
                pg.dma("sp", dc_, lambda e, c=c: e.dma_start(out=dd.ap()[c * 128:(c + 1) * 128, :], in_=xT[:, c * NTOK:(c + 1) * NTOK]), reads=[b for b in xB[c]])

        def sb(name, cols, dt, st=es):
            return st.enter_context(nc.sbuf_tensor("sb_" + name, [128, cols], dt))

        xT = sb("xT", 8 * NTOK, F32)
        xB = [[Buf(f"x{c}_{t}") for t in range(NTB)] for c in range(8)]
        hT = sb("hT", 8 * NTOK, BF16)
        hB = [[Buf(f"h{c}_{t}") for t in range(NTB)] for c in range(8)]
        norms = sb("norms", 56, F32)
        cst = sb("cst", 128 + 128 + 2048 + 512, F32)
        identf = cst[:, 0:128]
        ones_bf = sb("ones_bf", 128, BF16)
        negones_bf = sb("negones_bf", 128, BF16)
        negtri_bf = sb("negtri_bf", 128, BF16)
        flag = sb("flag", 1, F32)
        B_const = Buf("const")
        d_const = pg.dma_ctr("const")
        d_x = pg.dma_ctr("x")

        def XS(c, t, w=TB):
            return xT[:, c * NTOK + t * TB: c * NTOK + t * TB + w]

        def HS(c, t, w=TB, off=0):
            return hT[:, c * NTOK + t * TB + off: c * NTOK + t * TB + off + w]

        PSW = [es.enter_context(nc.psum_tensor(f"psw{i}", [128, 1024], F32)) for i in range(4)]
        PS = [PSW[i // 2][:, (i % 2) * 512:(i % 2 + 1) * 512] for i in range(8)]
        PB = [Buf(f"ps{i}", excl=True) for i in range(8)]

        pg.dma("sp", d_const, lambda e: e.dma_start(out=norms[:], in_=norms_d.ap()), writes=[B_const])
        pg.dma("sp", d_const, lambda e: e.dma_start(out=cst[:], in_=cst_d.ap()), writes=[B_const])
        pg.dma("sp", d_const, lambda e: e.dma_start(out=flag[:], in_=flag_d.ap()), writes=[B_const])
        for c in range(8):
            for t in range(NTB):
                pg.dma("sp", d_x, lambda e, c=c, t=t: e.dma_start(out=XS(c, t), in_=xT_d.ap()[c * 128:(c + 1) * 128, t * TB:(t + 1) * TB]),
                       writes=[xB[c][t]])
        pg.op("dve", lambda e: e.memset(ones_bf[:], 1.0 / 1024.0), writes=[B_const])
        pg.op("dve", lambda e: e.memset(negones_bf[:], -1.0), writes=[B_const])
        pg.op("dve", lambda e: e.tensor_scalar(out=negtri_bf[:], in0=cst[:, 128:256], scalar1=-1.0, scalar2=0.0, op0=ALU.mult, op1=ALU.add),
              reads=[B_const], writes=[B_const])

        def norm_prep(ni, st):
            sq = [sb(f"sq{ni}_{i}", 8 * TB, BF16, st) for i in range(2)]
            rstd = [sb(f"rstd{ni}_{i}", TB, F32, st) for i in range(2)]
            sqB = [[Buf() for _ in range(8)] for _ in range(2)]
            rB = [Buf() for _ in range(2)]

            def nb(t):
                q = t % 2
                for c in range(8):
                    if c % 4 != 3:
                        pg.op("act", lambda e, c=c, t=t, q=q: e.activation(out=sq[q][:, c * TB:(c + 1) * TB], in_=XS(c, t), func=AF.Square),
                              reads=[xB[c][t]], writes=[sqB[q][c]])
                    else:
                        pg.op("pool", lambda e, c=c, t=t, q=q: e.tensor_tensor(out=sq[q][:, c * TB:(c + 1) * TB], in0=XS(c, t), in1=XS(c, t), op=ALU.mult),
                              reads=[xB[c][t]], writes=[sqB[q][c]])
                for c in range(8):
                    pg.op("pe", lambda e, c=c, q=q: e.matmul(PS[7][:], lhsT=ones_bf[:], rhs=sq[q][:, c * TB:(c + 1) * TB], start=(c == 0), stop=(c == 7)),
                          reads=[sqB[q][c], B_const], writes=[PB[7]], inc=(c == 7))
                pg.op("act", lambda e, q=q: e.activation(out=rstd[q][:], in_=PS[7][:], func=AF.Ln, bias=EPS), reads=[PB[7]], writes=[rB[q]])
                pg.op("act", lambda e, q=q: e.activation(out=rstd[q][:], in_=rstd[q][:], func=AF.Exp, scale=-0.5), reads=[rB[q]], writes=[rB[q]])
                for c in range(8):
                    pg.op("dve", lambda e, c=c, t=t, q=q: e.scalar_tensor_tensor(out=HS(c, t), in0=XS(c, t), scalar=norms[:, ni * 8 + c: ni * 8 + c + 1],
                                                                              in1=rstd[q][:], op0=ALU.mult, op1=ALU.mult),
                          reads=[xB[c][t], rB[q], B_const], writes=[hB[c][t]])
            return nb

        def rmsnorm_to_hT(ni, st):
            nb = norm_prep(ni, st)
            for t in range(NTB):
                nb(t)

        FFG = 512
        groups = []
        f0 = 0
        while f0 < DFF:
            w = min(FFG, DFF - f0)
            groups.append((f0, w))
            f0 += w

        def ffn(fi, ni, wst):
            with ExitStack() as st:
                nb = norm_prep(ni, st)
                sil = [sb(f"sil{fi}_{i}", TB, F32, st) for i in range(2)]
                silB = [Buf() for _ in range(2)]
                gt = [sb(f"g{fi}_{i}", 4 * TB, BF16, st) for i in range(2)]
                gB = [[Buf() for _ in range(4)] for _ in range(2)]
                it = 0
                gi = 0

                def load_group(gidx):
                    f0, fw = groups[gidx]
                    slot = gidx % 2
                    wgS, wuS, wdS = wst["wg"][slot], wst["wu"][slot], wst["wd"][slot]
                    dctr = wst["ctr"][slot]
                    wb = wst["B"][slot]
                    nch = (fw + 127) // 128
                    for k in range(8):
                        pg.dma("pool", dctr, lambda e, k=k, f0=f0, fw=fw, wgS=wgS: e.dma_start(out=wgS[:, k * FFG:k * FFG + fw], in_=wg_d[fi].ap()[k * 128:(k + 1) * 128, f0:f0 + fw]),
                               writes=[wb["g"][k]])
                        pg.dma("pool", dctr, lambda e, k=k, f0=f0, fw=fw, wuS=wuS: e.dma_start(out=wuS[:, k * FFG:k * FFG + fw], in_=wu_d[fi].ap()[k * 128:(k + 1) * 128, f0:f0 + fw]),
                               writes=[wb["u"][k]])
                    for ci in range(nch):
                        cw = min(128, fw - ci * 128)
                        pg.dma("pool", dctr, lambda e, ci=ci, cw=cw, f0=f0, wdS=wdS: e.dma_start(out=wdS[0:cw, ci * D:(ci + 1) * D], in_=wd_d[fi].ap()[f0 + ci * 128:f0 + ci * 128 + cw, :]),
                               writes=[wb["d"][ci]])

                load_group(0)
                nb(0)
                for gidx, (f0, fw) in enumerate(groups):
                    slot = gidx % 2
                    wgS, wuS, wdS = wst["wg"][slot], wst["wu"][slot], wst["wd"][slot]
                    wb = wst["B"][slot]
                    nch = (fw + 127) // 128
                    if gidx + 1 < len(groups):
                        load_group(gidx + 1)
                    for t in range(NTB):
                        if gidx == 0 and t + 1 < NTB:
                            nb(t + 1)
                        gs = gi % 2
                        gi += 1
                        for ci in range(nch):
                            cw = min(128, fw - ci * 128)
                            pgate, pup = (it % 2), 2 + (it % 2)
                            s_ = it % 2
                            it += 1
                            for k in range(8):
                                pg.op("pe", lambda e, k=k, ci=ci, cw=cw, t=t, pgate=pgate, wgS=wgS: e.matmul(PS[pgate][0:cw, :], lhsT=wgS[:, k * FFG + ci * 128:k * FFG + ci * 128 + cw], rhs=HS(k, t), start=(k == 0), stop=(k == 7)),
                                      reads=[wb["g"][k], hB[k][t]], writes=[PB[pgate]], inc=(k == 7))
                            for k in range(8):
                                pg.op("pe", lambda e, k=k, ci=ci, cw=cw, t=t, pup=pup, wuS=wuS: e.matmul(PS[pup][0:cw, :], lhsT=wuS[:, k * FFG + ci * 128:k * FFG + ci * 128 + cw], rhs=HS(k, t), start=(k == 0), stop=(k == 7)),
                                      reads=[wb["u"][k], hB[k][t]], writes=[PB[pup]], inc=(k == 7))
                            pg.op("act", lambda e, cw=cw, pgate=pgate, s_=s_: e.activation(out=sil[s_][0:cw, :], in_=PS[pgate][0:cw, :], func=AF.Silu),
                                  reads=[PB[pgate]], writes=[silB[s_]])
                            pg.op("dve", lambda e, cw=cw, pup=pup, s_=s_, gs=gs, ci=ci: e.tensor_tensor(out=gt[gs][0:cw, ci * TB:(ci + 1) * TB], in0=PS[pup][0:cw, :], in1=sil[s_][0:cw, :], op=ALU.mult),
                                  reads=[PB[pup], silB[s_]], writes=[gB[gs][ci]])
                        for dc in range(8):
                            po = 4 + (dc % 2)
                            for ci in range(nch):
                                cw = min(128, fw - ci * 128)
                                pg.op("pe", lambda e, dc=dc, ci=ci, cw=cw, po=po, gs=gs, wdS=wdS, nch=nch: e.matmul(PS[po][:], lhsT=wdS[0:cw, ci * D + dc * 128:ci * D + (dc + 1) * 128], rhs=gt[gs][0:cw, ci * TB:(ci + 1) * TB], start=(ci == 0), stop=(ci == nch - 1)),
                                      reads=[wb["d"][ci], gB[gs][ci]], writes=[PB[po]], inc=(ci == nch - 1))
                            pg.op("dve", lambda e, dc=dc, t=t, po=po: e.scalar_tensor_tensor(out=XS(dc, t), in0=PS[po][:], scalar=0.5, in1=XS(dc, t), op0=ALU.mult, op1=ALU.add),
                                  reads=[PB[po]], writes=[xB[dc][t]])
                pg.barrier()

        def finish(dump_x=True):
            d_out = pg.dma_ctr("out")
            for c in range(8):
                pg.dma("sp", d_out, lambda e, c=c: e.dma_start(out=out_d.ap()[c * 128:(c + 1) * 128, :], in_=xT[:, c * NTOK:(c + 1) * NTOK]),
                       reads=[b for b in xB[c]])
            pg.streams["sp"].append(("wait", d_out.sem, d_out.val))
            if dbg:
                pass
            pg.emit()
            return nc


        def mk_wst(tag, st):
            return {"n": 0,
                    "wg": [sb(f"wgS{tag}{i}", 8 * FFG, BF16, st) for i in range(2)],
                    "wu": [sb(f"wuS{tag}{i}", 8 * FFG, BF16, st) for i in range(2)],
                    "wd": [sb(f"wdS{tag}{i}", 4 * D, BF16, st) for i in range(2)],
                    "B": [{"g": [Buf() for _ in range(8)], "u": [Buf() for _ in range(8)], "d": [Buf() for _ in range(4)]} for i in range(2)],
                    "ctr": [pg.dma_ctr(f"w{tag}{i}") for i in range(2)]}

        def run_ffn(fi, ni):
            with ExitStack() as st:
                ffn(fi, ni, mk_wst(fi, st))

        class Ring:
            def __init__(self, name, n, cols, dt, st):
                self.t = [sb(f"{name}{i}", cols, dt, st) for i in range(n)]
                self.b = [Buf(f"{name}{i}") for i in range(n)]
                self.c = [pg.dma_ctr(f"{name}{i}") for i in range(n)]
                self.i = 0

            def nxt(self):
                k = self.i % len(self.t)
                self.i += 1
                return self.t[k], self.b[k], self.c[k]

        def load_w(dst, dstB, ctr, src_ap_fn, nk, ncols, stride):
            for k in range(nk):
                pg.dma("pool", ctr, lambda e, k=k: e.dma_start(out=dst[:, k * stride:k * stride + ncols], in_=src_ap_fn(k)), writes=[dstB])

        run_ffn(0, 0)
        if stop_after == "ffn1_0":
            return finish()

        B_s1p = [[] for _ in range(4)]
        B_o1p = [Buf(f"o1_{i}") for i in range(4)]
        cc1 = [pg.dma_ctr(f"cc1_{i}") for i in range(4)]
        with ExitStack() as st:
            nb1 = norm_prep(1, st)
            wi = sb("wi", 8 * 2048, BF16, st)
            wiB = Buf("wi")
            wic = pg.dma_ctr("wi")
            load_w(wi, wiB, wic, lambda k: win_d.ap()[k * 128:(k + 1) * 128, :], 8, 2048, 2048)
            nb1(0)
            ring = Ring("stg1", 3, TB, BF16, st)
            nb = 0
            s1v = s1p[3].ap().rearrange("r (t f) -> (r t) f", f=512)

            def fm_piece(piece, interleave_norm):
                nonlocal_nb = [0]
                for t in range(NTB):
                    if interleave_norm and t + 1 < NTB:
                        nb1(t + 1)
                    for part in range(2):
                        for jl in range(2):
                            j = piece_j[piece] + jl
                            col0 = part * 1024 + j * 128
                            nbv = nbc[0]
                            pb = nbv % 2
                            for k in range(8):
                                pg.op("pe", lambda e, k=k, t=t, col0=col0, pb=pb: e.matmul(PS[pb][:], lhsT=wi[:, k * 2048 + col0:k * 2048 + col0 + 128], rhs=HS(k, t), start=(k == 0), stop=(k == 7)),
                                      reads=[wiB, hB[k][t]], writes=[PB[pb]], inc=(k == 7))
                            stg, stgB, stgC = ring.nxt()
                            scale = 0.125 if piece == 1 else 1.0
                            if nbv % 2 == 0:
                                pg.op("act", lambda e, stg=stg, pb=pb, scale=scale: e.activation(out=stg[:], in_=PS[pb][:], func=AF.Copy, scale=scale), reads=[PB[pb]], writes=[stgB])
                            else:
                                pg.op("dve", lambda e, stg=stg, pb=pb, scale=scale: e.tensor_scalar(out=stg[:], in0=PS[pb][:], scalar1=scale, scalar2=0.0, op0=ALU.mult, op1=ALU.add), reads=[PB[pb]], writes=[stgB])
                            row0 = part * 256 + jl * 128
                            b_ = Buf()
                            B_s1p[piece].append(b_)
                            pg.dma("sp", stgC, lambda e, stg=stg, row0=row0, t=t, piece=piece: e.dma_start(out=s1p[piece].ap()[row0:row0 + 128, t * TB:(t + 1) * TB], in_=stg[:]),
                                   reads=[stgB], writes=[b_])
                            nbc[0] += 1
                pg.cc(cc1[piece], lambda e, piece=piece: e.collective_compute("AllGather", ALU.bypass, replica_groups=PAIRS, ins=[s1p[piece].ap()], outs=[o1p[piece].ap()]),
                      reads=B_s1p[piece], writes=[B_o1p[piece]])

            piece_j = {0: 0, 1: 2, 2: 4}
            nbc = [0]
            fm_piece(1, True)
            fm_piece(2, False)
            for tt in range(16):
                pb = 2 + (tt % 2)
                for half in range(2):
                    c0 = half * 1024 + 768
                    for k in range(8):
                        pg.op("pe", lambda e, k=k, tt=tt, c0=c0, pb=pb, half=half: e.matmul(PS[pb][:, half * 256:(half + 1) * 256], lhsT=hT[:, k * NTOK + tt * 128:k * NTOK + (tt + 1) * 128], rhs=wi[:, k * 2048 + c0:k * 2048 + c0 + 256], start=(k == 0), stop=(k == 7)),
                              reads=[wiB, hB[k][tt // 4]], writes=[PB[pb]], inc=(k == 7))
                stg, stgB, stgC = ring.nxt()
                pg.op("act", lambda e, stg=stg, pb=pb: e.activation(out=stg[:], in_=PS[pb][:], func=AF.Copy), reads=[PB[pb]], writes=[stgB])
                b_ = Buf()
                B_s1p[3].append(b_)
                pg.dma("sp", stgC, lambda e, stg=stg, tt=tt: e.dma_start(out=s1v[tt * 128:(tt + 1) * 128, :], in_=stg[:]), reads=[stgB], writes=[b_])
            pg.cc(cc1[3], lambda e: e.collective_compute("AllGather", ALU.bypass, replica_groups=PAIRS, ins=[s1p[3].ap()], outs=[o1p[3].ap()]),
                  reads=B_s1p[3], writes=[B_o1p[3]])
            fm_piece(0, False)
            pg.barrier()
        if stop_after == "proj0":
            return finish()

        B_s2 = [Buf(f"s2_{i}") for i in range(64)]
        B_o2 = [Buf("o2z"), Buf("o2a")]
        B_s2a = [Buf(f"s2a_{i}") for i in range(16)]
        cc2 = [pg.dma_ctr(f"cc2_{i}") for i in range(2)]
        ns2 = [0]
        with ExitStack() as st:
            QT = hT[:, 0:2 * SEQ]
            KT = hT[:, 2 * SEQ:4 * SEQ]
            VT = sb("VT", 32 * 256, BF16, st)
            qkvB = Buf("qkv")
            qkvC = pg.dma_ctr("qkv")
            maskb = sb("maskb", 4 * 512, BF16, st)
            maskB = Buf("mask")
            pg.op("dve", lambda e: e.tensor_copy(out=maskb[:], in_=cst[:, 256:256 + 2048]), reads=[B_const], writes=[maskB])
            for s_ in range(2):
                o1v = o1p[3].ap()[s_ * 512:(s_ + 1) * 512, :].rearrange("r (t f) -> (r t) f", f=512)
                for j, dst in ((2, QT), (3, QT), (4, KT), (5, KT)):
                    cidx = j % 2
                    o1f = o1p[j // 2].ap()[s_ * 512:(s_ + 1) * 512, :].rearrange("(a q) t -> a q t", a=2)
                    pg.dma("pool", qkvC, lambda e, s_=s_, j=j, dst=dst, cidx=cidx, o1f=o1f: e.dma_start(
                        out=dst[:, cidx * SEQ + s_ * NTOK:cidx * SEQ + (s_ + 1) * NTOK],
                        in_=o1f[bass.ds(PART(e, s_), 1), cidx * 128:(cidx + 1) * 128, :].rearrange("a p t -> (a p) t")),
                        reads=[B_o1p[j // 2]], writes=[qkvB])
                pg.dma("pool", qkvC, lambda e, s_=s_, o1v=o1v: e.dma_start(
                    out=VT[:, s_ * 4096:(s_ + 1) * 4096].rearrange("p (n f) -> p n f", f=256),
                    in_=o1v.rearrange("(n p) f -> p n f", p=128)[:, :, bass.ds(PART(e, s_) * 256, 256)]),
                    reads=[B_o1p[3]], writes=[qkvB])
            mask2 = sb("mask2", 4 * 1024, BF16, st)
            for o in range(4):
                for hh in range(2):
                    pg.op("dve", lambda e, o=o, hh=hh: e.tensor_copy(out=mask2[:, o * 1024 + hh * 512:o * 1024 + (hh + 1) * 512], in_=maskb[:, o * 512:(o + 1) * 512]),
                          reads=[maskB], writes=[maskB])
            W2 = 2 * TB
            Et = [sb(f"E{i}", W2, F32, st) for i in range(2)]
            St = [[sb(f"S{i}_{k}", W2, BF16, st) for k in range(2)] for i in range(2)]
            Wt = [[sb(f"W{i}_{k}", W2, BF16, st) for k in range(2)] for i in range(2)]
            At = [sb(f"A{i}", W2, BF16, st) for i in range(2)]
            EB = [Buf() for _ in range(2)]
            SB_ = [[Buf() for _ in range(2)] for _ in range(2)]
            WB = [[Buf() for _ in range(2)] for _ in range(2)]
            AB = [Buf() for _ in range(2)]
            oring = Ring("ostg", 3, TB, BF16, st)
            steps = []
            for qb in range(8):
                kbs = list(range(4 * qb + 3, -1, -1))
                for idx, kb in enumerate(kbs):
                    steps.append((qb, kb, idx == 0, kb == 0))

            def zmm(i, p):
                qb, kb, first, last = steps[i]
                for hl in range(2):
                    r0 = hl * 64
                    pg.op("pe", lambda e, p=p, hl=hl, r0=r0, kb=kb, qb=qb: e.matmul(PS[2 * p + hl], lhsT=KT[r0:r0 + 64, p * SEQ + kb * 128:p * SEQ + (kb + 1) * 128],
                                                                                 rhs=QT[r0:r0 + 64, p * SEQ + qb * TB:p * SEQ + (qb + 1) * TB], start=True, stop=False),
                          reads=[qkvB], writes=[PB[2 * p + hl]])

            def es_ops(i, p):
                qb, kb, first, last = steps[i]
                k = i % 2
                pg.op("act", lambda e, p=p: e.activation(out=Et[p][:], in_=PSW[p][:], func=AF.Exp), reads=[PB[2 * p], PB[2 * p + 1]], writes=[EB[p]])
                pg.op("act", lambda e, p=p, k=k: e.activation(out=St[p][k][:], in_=Et[p][:], func=AF.Ln, bias=1.0), reads=[EB[p]], writes=[SB_[p][k]])
                o = kb - 4 * qb
                if o >= 0:
                    pg.op("dve", lambda e, p=p, k=k, o=o: e.tensor_tensor(out=St[p][k][:], in0=St[p][k][:], in1=mask2[:, o * 1024:(o + 1) * 1024], op=ALU.mult),
                          reads=[maskB], writes=[SB_[p][k]])

            def tri_ops(i, p):
                qb, kb, first, last = steps[i]
                k = i % 2
                for hl in range(2):
                    hs = slice(hl * TB, (hl + 1) * TB)
                    pg.op("pe", lambda e, p=p, hl=hl, k=k, hs=hs, first=first: e.matmul(PS[2 * p + hl], lhsT=negtri_bf[:], rhs=St[p][k][:, hs], start=False, stop=first),
                          reads=[B_const, SB_[p][k]], writes=[PB[2 * p + hl]], inc=first)
                    if not first:
                        pg.op("pe", lambda e, p=p, hl=hl, hs=hs: e.matmul(PS[2 * p + hl], lhsT=negones_bf[:], rhs=At[p][:, hs], start=False, stop=True),
                              reads=[B_const, AB[p]], writes=[PB[2 * p + hl]])

            def w_ops(i, p):
                qb, kb, first, last = steps[i]
                k = i % 2
                pg.op("act", lambda e, p=p, k=k: e.activation(out=Wt[p][k][:], in_=PSW[p][:], func=AF.Exp), reads=[PB[2 * p], PB[2 * p + 1]], writes=[WB[p][k]])
                o = kb - 4 * qb
                if o >= 0:
                    pg.op("dve", lambda e, p=p, k=k, o=o: e.tensor_tensor(out=Wt[p][k][:], in0=Wt[p][k][:], in1=mask2[:, o * 1024:(o + 1) * 1024], op=ALU.mult),
                          reads=[maskB], writes=[WB[p][k]])

            def pv_ops(i, p):
                qb, kb, first, last = steps[i]
                k = i % 2
                ob = 4 + p + 2 * (qb % 2)
                for hl in range(2):
                    h = 2 * p + hl
                    hs = slice(hl * TB, (hl + 1) * TB)
                    pg.op("pe", lambda e, ob=ob, hl=hl, h=h, kb=kb, p=p, k=k, hs=hs, first=first, last=last: e.matmul(PS[ob][hl * 64:(hl + 1) * 64, :], lhsT=VT[:, kb * 256 + h * 64:kb * 256 + (h + 1) * 64], rhs=Wt[p][k][:, hs], start=first, stop=last),
                          reads=[qkvB, WB[p][k]], writes=[PB[ob]])
                if not last:
                    eng = "dve" if p == 0 else "pool"
                    if first:
                        pg.op(eng, lambda e, p=p, k=k: e.tensor_copy(out=At[p][:], in_=St[p][k][:]), reads=[SB_[p][k]], writes=[AB[p]])
                    else:
                        pg.op(eng, lambda e, p=p, k=k: e.tensor_tensor(out=At[p][:], in0=At[p][:], in1=St[p][k][:], op=ALU.add), reads=[SB_[p][k]], writes=[AB[p]])
                else:
                    stg, stgB, stgC = oring.nxt()
                    pg.op("dve", lambda e, stg=stg, ob=ob: e.tensor_copy(out=stg[:], in_=PS[ob]), reads=[PB[ob]], writes=[stgB])
                    pg.dma("sp", stgC, lambda e, stg=stg, qb=qb, p=p: e.dma_start(out=s2p[1].ap()[p * 128:(p + 1) * 128, qb * TB:(qb + 1) * TB], in_=stg[:]),
                           reads=[stgB], writes=[B_s2a[qb * 2 + p]])

            n_st = len(steps)
            for p in range(2):
                zmm(0, p)
            for p in range(2):
                es_ops(0, p)
            for i in range(n_st):
                for p in range(2):
                    tri_ops(i, p)
                for p in range(2):
                    w_ops(i, p)
                for p in range(2):
                    if i + 1 < n_st:
                        zmm(i + 1, p)
                    pv_ops(i, p)
                if i + 1 < n_st:
                    for p in range(2):
                        es_ops(i + 1, p)
            pg.cc(cc2[1], lambda e: e.collective_compute("AllGather", ALU.bypass, replica_groups=PAIRS, ins=[s2p[1].ap()], outs=[o2p[1].ap()]),
                  reads=B_s2a, writes=[B_o2[1]])
            pg.barrier()

        with ExitStack() as st:
            UT = hT[:, 0:2 * SEQ]
            uB = Buf("u")
            uC = pg.dma_ctr("u")
            for s_ in range(2):
                o1f = o1p[0].ap()[s_ * 512:(s_ + 1) * 512, :].rearrange("(a q) t -> a q t", a=2)
                for j in (0, 1):
                    pg.dma("sp", uC, lambda e, s_=s_, j=j, o1f=o1f: e.dma_start(
                        out=UT[:, j * SEQ + s_ * NTOK:j * SEQ + (s_ + 1) * NTOK],
                        in_=o1f[bass.ds(PART(e, s_), 1), j * 128:(j + 1) * 128, :].rearrange("a p t -> (a p) t")),
                        reads=[B_o1p[0]], writes=[uB])
            L = sb("lam", 40, F32, st)
            s5d = sb("s5d", 2, F32, st)
            pcB = Buf("s5pre")
            pcC = pg.dma_ctr("s5pre")
            pg.dma("sp", pcC, lambda e: e.dma_start(out=L[:], in_=lam_d.ap()), writes=[pcB])
            pg.dma("sp", pcC, lambda e: e.dma_start(out=s5d[:], in_=s5d_d.ap()), writes=[pcB])
            sc = sb("s5sc", 16 * 8, F32, st)

            def SC(i):
                return sc[:, i * 8:(i + 1) * 8]
            lr, li, ld = L[:, 0:8], L[:, 8:16], L[:, 16:24]
            DT, MAG, TH, SN1, CS1, ABR, ABI, RDEN, NR, CRE, CIM, T0, T1 = [SC(i) for i in range(13)]

            def dv(fn, **kw):
                pg.op("dve", fn, reads=[pcB], writes=[pcB])

            def ac(fn):
                pg.op("act", fn, reads=[pcB], writes=[pcB])
            ac(lambda e: e.activation(out=DT, in_=ld, func=AF.Exp))
            dv(lambda e: e.tensor_tensor(out=T0, in0=lr, in1=DT, op=ALU.mult))
            ac(lambda e: e.activation(out=MAG, in_=T0, func=AF.Exp))
            dv(lambda e: e.tensor_tensor(out=TH, in0=li, in1=DT, op=ALU.mult))
            MAGIC = 12582912.0

            def range_sin(dst, x, tmp):
                dv(lambda e: e.tensor_scalar(out=tmp, in0=x, scalar1=1.0 / (2 * PI), scalar2=MAGIC, op0=ALU.mult, op1=ALU.add))
                dv(lambda e: e.tensor_scalar(out=tmp, in0=tmp, scalar1=-MAGIC, scalar2=0.0, op0=ALU.add, op1=ALU.add))
                dv(lambda e: e.scalar_tensor_tensor(out=x, in0=tmp, scalar=-2 * PI, in1=x, op0=ALU.mult, op1=ALU.add))
                dv(lambda e: e.tensor_scalar(out=x, in0=x, scalar1=-PI, scalar2=PI, op0=ALU.max, op1=ALU.min))
                ac(lambda e: e.activation(out=dst, in_=x, func=AF.Sin))
            T2 = SC(13)
            dv(lambda e: e.tensor_scalar(out=T0, in0=TH, scalar1=1.0, scalar2=0.0, op0=ALU.mult, op1=ALU.add))
            range_sin(SN1, T0, T2)
            dv(lambda e: e.tensor_scalar(out=T0, in0=TH, scalar1=1.0, scalar2=0.5 * PI, op0=ALU.mult, op1=ALU.add))
            range_sin(CS1, T0, T2)
            dv(lambda e: e.tensor_tensor(out=ABR, in0=MAG, in1=CS1, op=ALU.mult))
            dv(lambda e: e.tensor_tensor(out=ABI, in0=MAG, in1=SN1, op=ALU.mult))
            dv(lambda e: e.tensor_tensor(out=T0, in0=lr, in1=lr, op=ALU.mult))
            dv(lambda e: e.tensor_tensor(out=T1, in0=li, in1=li, op=ALU.mult))
            dv(lambda e: e.tensor_tensor(out=T0, in0=T0, in1=T1, op=ALU.add))
            dv(lambda e: e.reciprocal(out=RDEN, in_=T0))
            dv(lambda e: e.tensor_scalar(out=NR, in0=ABR, scalar1=-1.0, scalar2=0.0, op0=ALU.add, op1=ALU.add))
            dv(lambda e: e.tensor_tensor(out=T0, in0=NR, in1=lr, op=ALU.mult))
            dv(lambda e: e.tensor_tensor(out=T1, in0=ABI, in1=li, op=ALU.mult))
            dv(lambda e: e.tensor_tensor(out=T0, in0=T0, in1=T1, op=ALU.add))
            dv(lambda e: e.tensor_tensor(out=CRE, in0=T0, in1=RDEN, op=ALU.mult))
            dv(lambda e: e.tensor_tensor(out=T0, in0=ABI, in1=lr, op=ALU.mult))
            dv(lambda e: e.tensor_tensor(out=T1, in0=NR, in1=li, op=ALU.mult))
            dv(lambda e: e.tensor_tensor(out=T0, in0=T0, in1=T1, op=ALU.subtract))
            dv(lambda e: e.tensor_tensor(out=CIM, in0=T0, in1=RDEN, op=ALU.mult))
            CSt = sb("CSt", 8 * TB, F32, st)
            SNt = sb("SNt", 8 * TB, F32, st)
            tmpA = sb("tmpA", TB, F32, st)
            tmpA2 = sb("tmpA2", TB, F32, st)
            iota = cst[:, 256 + 2048:256 + 2048 + 512]
            for j in range(8):
                for tab, ph in ((SNt, 0.0), (CSt, 0.5 * PI)):
                    dv(lambda e, j=j, ph=ph: e.tensor_scalar(out=tmpA[:], in0=iota, scalar1=TH[:, j:j + 1], scalar2=ph, op0=ALU.mult, op1=ALU.add))
                    range_sin(tab[:, j * TB:(j + 1) * TB], tmpA[:], tmpA2[:])
            BDr = sb("BDr", 8 * 128, BF16, st)
            BDi = sb("BDi", 8 * 128, BF16, st)
            CTr = sb("CTr", 8 * 128, BF16, st)
            CTi = sb("CTi", 8 * 128, BF16, st)
            st2 = ExitStack()
            BRr = sb("BRr", 8 * 128, F32, st2)
            BRi = sb("BRi", 8 * 128, F32, st2)
            CRr = sb("CRr", 8 * 128, F32, st2)
            CRi = sb("CRi", 8 * 128, F32, st2)
            bcB = Buf("bc")
            bcC = pg.dma_ctr("bc")

            bc_list = []

            def fresh():
                b_ = Buf()
                bc_list.append(b_)
                return b_
            for tl in (BRr, BRi, CRr, CRi):
                pg.op("pool", lambda e, tl=tl: e.memset(tl[:], 0.0), writes=[bcB])
            for j in range(8):
                jj = j % 4
                for g in range(2):
                    for tl, src in ((BRr, bre_d), (BRi, bim_d)):
                        pg.dma("sp", bcC, lambda e, tl=tl, src=src, j=j, jj=jj, g=g: e.dma_start(out=tl[64 * g:64 * g + 64, j * 128 + 32 * jj + 16 * g:j * 128 + 32 * jj + 16 * g + 16], in_=src.ap()[2 * j + g, :, :]),
                               reads=[bcB], writes=[fresh()])
                    for tl, src in ((CRr, cre_d), (CRi, cim_d)):
                        pg.dma("sp", bcC, lambda e, tl=tl, src=src, j=j, jj=jj, g=g: e.dma_start(out=tl[32 * jj + 16 * g:32 * jj + 16 * g + 16, j * 128 + 64 * g:j * 128 + 64 * g + 64], in_=src.ap()[2 * j + g, :, :]),
                               reads=[bcB], writes=[fresh()])
            bbr = sb("bbr", 128, F32, st2)
            bbi = sb("bbi", 128, F32, st2)
            tmpB = sb("tmpB", 128, F32, st2)
            bcLast = bc_list[-1]
            bbB = Buf("bb")
            matB = Buf("mats")
            for j in range(8):
                sl = slice(j * 128, (j + 1) * 128)
                pg.op("dve", lambda e, sl=sl, j=j: e.tensor_scalar(out=tmpB[:], in0=BRi[:, sl], scalar1=CIM[:, j:j + 1], scalar2=0.0, op0=ALU.mult, op1=ALU.add), reads=[bcB, bcLast, pcB], writes=[bbB])
                pg.op("dve", lambda e, sl=sl, j=j: e.scalar_tensor_tensor(out=bbr[:], in0=BRr[:, sl], scalar=CRE[:, j:j + 1], in1=tmpB[:], op0=ALU.mult, op1=ALU.subtract), reads=[bcB, bcLast, pcB], writes=[bbB])
                pg.op("dve", lambda e, sl=sl, j=j: e.tensor_scalar(out=tmpB[:], in0=BRr[:, sl], scalar1=CIM[:, j:j + 1], scalar2=0.0, op0=ALU.mult, op1=ALU.add), reads=[bcB, bcLast, pcB], writes=[bbB])
                pg.op("dve", lambda e, sl=sl, j=j: e.scalar_tensor_tensor(out=bbi[:], in0=BRi[:, sl], scalar=CRE[:, j:j + 1], in1=tmpB[:], op0=ALU.mult, op1=ALU.add), reads=[bcB, bcLast, pcB], writes=[bbB])
                for src, dst, neg, sliced in ((bbr, BDr, False, False), (bbi, BDi, False, False), (CRr, CTr, False, True), (CRi, CTi, True, True)):
                    srcap = (lambda src=src, sl=sl, sliced=sliced: src[:, sl] if sliced else src[:])
                    pg.op("pe", lambda e, srcap=srcap: e.transpose(out=PS[6][:, 0:128], in_=srcap(), identity=identf), reads=[bbB, bcB, bcLast, B_const], writes=[PB[6]])
                    pg.op("act", lambda e, dst=dst, sl=sl, neg=neg: e.activation(out=dst[:, sl], in_=PS[6][:, 0:128], func=AF.Copy, scale=(-1.0 if neg else 1.0)), reads=[PB[6]], writes=[matB])
            pg.barrier()
            st2.close()
            hrp = sb("hrp", 8, F32, st)
            hip = sb("hip", 8, F32, st)
            hpB = [Buf() for _ in range(8)]
            pg.op("dve", lambda e: e.memset(hrp[:], 0.0), writes=hpB)
            pg.op("dve", lambda e: e.memset(hip[:], 0.0), writes=hpB)
            nbuf = 3
            t1 = [sb(f"t1_{i}", TB, F32, st) for i in range(nbuf)]
            t2 = [sb(f"t2_{i}", TB, F32, st) for i in range(nbuf)]
            t3 = [sb(f"t3_{i}", TB, F32, st) for i in range(nbuf)]
            t4 = [sb(f"t4_{i}", TB, F32, st) for i in range(nbuf)]
            gr = [sb(f"gr_{i}", TB, F32, st) for i in range(nbuf)]
            gi_ = [sb(f"gi_{i}", TB, F32, st) for i in range(nbuf)]
            hr = [sb(f"hr_{i}", TB, BF16, st) for i in range(nbuf)]
            hi = [sb(f"hi_{i}", TB, BF16, st) for i in range(nbuf)]
            tc_ = [sb(f"tcar{i}", 4, F32, st) for i in range(nbuf)]
            wB_ = [[Buf() for _ in range(12)] for _ in range(nbuf)]
            ysb = sb("ysb", TB, F32, st)
            y2 = sb("y2", TB, F32, st)
            yth = sb("yth", TB, F32, st)
            yB = Buf("y")
            zring = Ring("zstg", 2, TB, BF16, st)
            units = [(c, k, jj) for c in range(2) for k in range(8) for jj in range(4)]

            def uvars(u):
                c, k, jj = units[u]
                j = 4 * c + jj
                b = u % nbuf
                px, pi_ = (0, 1) if u % 2 == 0 else (2, 3)
                return dict(c=c, k=k, jj=jj, j=j, b=b, px=px, pi_=pi_, W_=wB_[b], sl=slice(j * 128, (j + 1) * 128), tsl=slice(j * TB, (j + 1) * TB),
                            usl=slice(c * SEQ + k * TB, c * SEQ + (k + 1) * TB), yb_=4 + (k % 2))

            def stage1(u):
                v = uvars(u)
                c, k, jj, j, b, px, pi_, W_, sl, tsl, usl, yb_ = (v[x] for x in ("c", "k", "jj", "j", "b", "px", "pi_", "W_", "sl", "tsl", "usl", "yb_"))
                pg.op("pe", lambda e, px=px, sl=sl, usl=usl: e.matmul(PS[px], lhsT=BDr[:, sl], rhs=UT[:, usl], start=True, stop=True), reads=[matB, uB], writes=[PB[px]])
                pg.op("pe", lambda e, pi_=pi_, sl=sl, usl=usl: e.matmul(PS[pi_], lhsT=BDi[:, sl], rhs=UT[:, usl], start=True, stop=True), reads=[matB, uB], writes=[PB[pi_]])
                pg.op("dve", lambda e, b=b, tsl=tsl, px=px: e.tensor_tensor(out=t1[b][:], in0=PS[px], in1=CSt[:, tsl], op=ALU.mult), reads=[PB[px], pcB], writes=[W_[2]])
                pg.op("dve", lambda e, b=b, tsl=tsl, pi_=pi_: e.tensor_tensor(out=t2[b][:], in0=PS[pi_], in1=SNt[:, tsl], op=ALU.mult), reads=[PB[pi_], pcB], writes=[W_[3]])
                pg.op("dve", lambda e, b=b, tsl=tsl, pi_=pi_: e.tensor_tensor(out=t3[b][:], in0=PS[pi_], in1=CSt[:, tsl], op=ALU.mult), reads=[PB[pi_], pcB], writes=[W_[4]])
                pg.op("dve", lambda e, b=b, tsl=tsl, px=px: e.tensor_tensor(out=t4[b][:], in0=PS[px], in1=SNt[:, tsl], op=ALU.mult), reads=[PB[px], pcB], writes=[W_[5]])
                pg.op("pool", lambda e, b=b: e.tensor_tensor(out=t1[b][:], in0=t1[b][:], in1=t2[b][:], op=ALU.add), reads=[W_[3]], writes=[W_[2]])
                pg.op("pool", lambda e, b=b: e.tensor_tensor(out=t3[b][:], in0=t3[b][:], in1=t4[b][:], op=ALU.subtract), reads=[W_[5]], writes=[W_[4]])

            def stage2(u):
                v = uvars(u)
                c, k, jj, j, b, px, pi_, W_, sl, tsl, usl, yb_ = (v[x] for x in ("c", "k", "jj", "j", "b", "px", "pi_", "W_", "sl", "tsl", "usl", "yb_"))
                pg.op("dve", lambda e, b=b, j=j: e.tensor_tensor_scan(out=gr[b][:], data0=MAG[:, j:j + 1].to_broadcast([128, TB]), data1=t1[b][:], initial=hrp[:, j:j + 1], op0=ALU.mult, op1=ALU.add),
                      reads=[W_[2], pcB, hpB[j]], writes=[W_[6]])
                pg.op("dve", lambda e, b=b, j=j: e.tensor_tensor_scan(out=gi_[b][:], data0=MAG[:, j:j + 1].to_broadcast([128, TB]), data1=t3[b][:], initial=hip[:, j:j + 1], op0=ALU.mult, op1=ALU.add),
                      reads=[W_[4], pcB, hpB[j]], writes=[W_[7]])
                cl = j * TB + TB - 1
                tcb = tc_[b]
                pg.op("pool", lambda e, b=b, cl=cl, tcb=tcb: e.tensor_tensor(out=tcb[:, 0:1], in0=gi_[b][:, TB - 1:TB], in1=SNt[:, cl:cl + 1], op=ALU.mult), reads=[W_[7], pcB], writes=[W_[10]])
                pg.op("pool", lambda e, b=b, cl=cl, tcb=tcb: e.tensor_tensor(out=tcb[:, 1:2], in0=gr[b][:, TB - 1:TB], in1=CSt[:, cl:cl + 1], op=ALU.mult), reads=[W_[6], pcB], writes=[W_[10]])
                pg.op("pool", lambda e, j=j, tcb=tcb: e.tensor_tensor(out=hrp[:, j:j + 1], in0=tcb[:, 1:2], in1=tcb[:, 0:1], op=ALU.subtract), reads=[W_[10]], writes=[hpB[j]])
                pg.op("pool", lambda e, b=b, cl=cl, tcb=tcb: e.tensor_tensor(out=tcb[:, 2:3], in0=gr[b][:, TB - 1:TB], in1=SNt[:, cl:cl + 1], op=ALU.mult), reads=[W_[6], pcB], writes=[W_[10]])
                pg.op("pool", lambda e, b=b, cl=cl, tcb=tcb: e.tensor_tensor(out=tcb[:, 3:4], in0=gi_[b][:, TB - 1:TB], in1=CSt[:, cl:cl + 1], op=ALU.mult), reads=[W_[7], pcB], writes=[W_[10]])
                pg.op("pool", lambda e, j=j, tcb=tcb: e.tensor_tensor(out=hip[:, j:j + 1], in0=tcb[:, 3:4], in1=tcb[:, 2:3], op=ALU.add), reads=[W_[10]], writes=[hpB[j]])
                pg.op("dve", lambda e, b=b, tsl=tsl: e.tensor_tensor(out=t1[b][:], in0=gr[b][:], in1=CSt[:, tsl], op=ALU.mult), reads=[W_[6], pcB], writes=[W_[2]])
                pg.op("dve", lambda e, b=b, tsl=tsl: e.tensor_tensor(out=t2[b][:], in0=gi_[b][:], in1=SNt[:, tsl], op=ALU.mult), reads=[W_[7], pcB], writes=[W_[3]])
                pg.op("dve", lambda e, b=b: e.tensor_tensor(out=hr[b][:], in0=t1[b][:], in1=t2[b][:], op=ALU.subtract), reads=[W_[2], W_[3]], writes=[W_[8]])
                pg.op("pool", lambda e, b=b, tsl=tsl: e.tensor_tensor(out=t3[b][:], in0=gi_[b][:], in1=CSt[:, tsl], op=ALU.mult), reads=[W_[7], pcB], writes=[W_[4]])
                pg.op("pool", lambda e, b=b, tsl=tsl: e.tensor_tensor(out=t4[b][:], in0=gr[b][:], in1=SNt[:, tsl], op=ALU.mult), reads=[W_[6], pcB], writes=[W_[5]])
                pg.op("pool", lambda e, b=b: e.tensor_tensor(out=hi[b][:], in0=t3[b][:], in1=t4[b][:], op=ALU.add), reads=[W_[4], W_[5]], writes=[W_[9]])
                pg.op("pe", lambda e, yb_=yb_, sl=sl, b=b, jj=jj: e.matmul(PS[yb_], lhsT=CTr[:, sl], rhs=hr[b][:], start=(jj == 0), stop=False), reads=[matB, W_[8]], writes=[PB[yb_]])
                pg.op("pe", lambda e, yb_=yb_, sl=sl, b=b, jj=jj: e.matmul(PS[yb_], lhsT=CTi[:, sl], rhs=hi[b][:], start=False, stop=(jj == 3)), reads=[matB, W_[9]], writes=[PB[yb_]])

            def tail(u):
                v = uvars(u)
                c, k, jj, j, b, px, pi_, W_, sl, tsl, usl, yb_ = (v[x] for x in ("c", "k", "jj", "j", "b", "px", "pi_", "W_", "sl", "tsl", "usl", "yb_"))
                pg.op("dve", lambda e, yb_=yb_, usl=usl, c=c: e.scalar_tensor_tensor(out=ysb[:], in0=UT[:, usl], scalar=s5d[:, c:c + 1], in1=PS[yb_][:], op0=ALU.mult, op1=ALU.add),
                      reads=[PB[yb_], uB, pcB], writes=[yB])
                pg.op("pool", lambda e: e.tensor_tensor(out=y2[:], in0=ysb[:], in1=ysb[:], op=ALU.mult), reads=[yB], writes=[yB])
                pg.op("pool", lambda e: e.tensor_scalar(out=y2[:], in0=y2[:], scalar1=0.044715, scalar2=1.0, op0=ALU.mult, op1=ALU.add), reads=[yB], writes=[yB])
                pg.op("pool", lambda e: e.tensor_tensor(out=y2[:], in0=y2[:], in1=ysb[:], op=ALU.mult), reads=[yB], writes=[yB])
                pg.op("act", lambda e: e.activation(out=yth[:], in_=y2[:], func=AF.Tanh, scale=0.7978845608028654), reads=[yB], writes=[yB])
                pg.op("pool", lambda e: e.tensor_scalar(out=ysb[:], in0=ysb[:], scalar1=0.5, scalar2=0.0, op0=ALU.mult, op1=ALU.add), reads=[yB], writes=[yB])
                stg, stgB, stgC = zring.nxt()
                pg.op("dve", lambda e, stg=stg: e.scalar_tensor_tensor(out=stg[:], in0=yth[:], scalar=1.0, in1=ysb[:], op0=ALU.add, op1=ALU.mult), reads=[yB], writes=[stgB])
                pg.dma("sp", stgC, lambda e, stg=stg, c=c, k=k: e.dma_start(out=s2p[0].ap()[c * 128:(c + 1) * 128, k * TB:(k + 1) * TB], in_=stg[:]),
                       reads=[stgB], writes=[B_s2[ns2[0] % 64]])
                ns2[0] += 1

            stage1(0)
            for u in range(len(units)):
                if u + 1 < len(units):
                    stage1(u + 1)
                stage2(u)
                if units[u][2] == 3:
                    tail(u)
            pg.cc(cc2[0], lambda e: e.collective_compute("AllGather", ALU.bypass, replica_groups=PAIRS, ins=[s2p[0].ap()], outs=[o2p[0].ap()]),
                  reads=B_s2, writes=[B_o2[0]])
            pg.barrier()
        if stop_after == "mixB":
            dC = pg.dma_ctr("dbg")
            pg.dma("sp", dC, lambda e: e.dma_start(out=dbg_d.ap()[0:512, :], in_=o2p[0].ap()), reads=B_o2)
            pg.dma("sp", dC, lambda e: e.dma_start(out=dbg_d.ap()[512:1024, :], in_=o2p[1].ap()), reads=B_o2)
            pg.streams["sp"].append(("wait", dC.sem, 32))
            return finish()

        with ExitStack() as st:
            mC = pg.dma_ctr("mixin")
            for slot in range(2):
                for kind in range(2):
                    for ci in range(2):
                        row0 = slot * 256 + ci * 128
                        hc = kind * 4 + slot * 2 + ci
                        pg.dma("sp", mC, lambda e, row0=row0, hc=hc, kind=kind: e.dma_start(out=hT[:, hc * NTOK:(hc + 1) * NTOK],
                                                                               in_=o2p[kind].ap()[row0:row0 + 128, bass.ds(PART(e, 0) * NTOK, NTOK)]),
                               reads=[B_o2[kind]], writes=hB[hc])
            wgl = sb("wgl", 4 * 512, BF16, st)
            wot = sb("wot", 8 * D, BF16, st)
            wcB = Buf("wc")
            wcC = pg.dma_ctr("wc")
            load_w(wgl, wcB, wcC, lambda k: wglu_d.ap()[k * 128:(k + 1) * 128, :], 4, 512, 512)
            load_w(wot, wcB, wcC, lambda k: wout_d.ap()[k * 128:(k + 1) * 128, :], 8, D, D)
            sg = [sb(f"sg{i}", TB, F32, st) for i in range(2)]
            sgB = [Buf() for _ in range(2)]
            yab = sb("yab", 4 * TB, BF16, st)
            yaB = [Buf() for _ in range(4)]
            n_ = 0
            for t in range(NTB):
                for mc in range(4):
                    pb = n_ % 2
                    s_ = n_ % 2
                    n_ += 1
                    for kc in range(4):
                        pg.op("pe", lambda e, pb=pb, kc=kc, mc=mc, t=t: e.matmul(PS[pb][:], lhsT=wgl[:, kc * 512 + mc * 128:kc * 512 + (mc + 1) * 128], rhs=HS(kc, t), start=(kc == 0), stop=(kc == 3)),
                              reads=[wcB, hB[kc][t]], writes=[PB[pb]], inc=(kc == 3))
                    pg.op("act", lambda e, pb=pb, s_=s_: e.activation(out=sg[s_][:], in_=PS[pb][:], func=AF.Tanh, scale=0.5), reads=[PB[pb]], writes=[sgB[s_]])
                    pg.op("dve", lambda e, s_=s_: e.tensor_scalar(out=sg[s_][:], in0=sg[s_][:], scalar1=0.5, scalar2=0.5, op0=ALU.mult, op1=ALU.add), reads=[], writes=[sgB[s_]])
                    pg.op("dve", lambda e, s_=s_, mc=mc, t=t: e.tensor_tensor(out=yab[:, mc * TB:(mc + 1) * TB], in0=sg[s_][:], in1=HS(mc, t), op=ALU.mult), reads=[sgB[s_], hB[mc][t]], writes=[yaB[mc]])
                for dc in range(8):
                    po = 2 + (dc % 2)
                    for i in range(8):
                        if i < 4:
                            pg.op("pe", lambda e, po=po, i=i, dc=dc: e.matmul(PS[po][:], lhsT=wot[:, i * D + dc * 128:i * D + (dc + 1) * 128], rhs=yab[:, i * TB:(i + 1) * TB], start=(i == 0), stop=False),
                                  reads=[wcB, yaB[i]], writes=[PB[po]], inc=False)
                        else:
                            pg.op("pe", lambda e, po=po, i=i, dc=dc, t=t: e.matmul(PS[po][:], lhsT=wot[:, i * D + dc * 128:i * D + (dc + 1) * 128], rhs=HS(i, t), start=False, stop=(i == 7)),
                                  reads=[wcB, hB[i][t]], writes=[PB[po]], inc=(i == 7))
                    pg.op("dve", lambda e, dc=dc, t=t, po=po: e.scalar_tensor_tensor(out=XS(dc, t), in0=PS[po][:], scalar=1.0, in1=XS(dc, t), op0=ALU.mult, op1=ALU.add),
                          reads=[PB[po]], writes=[xB[dc][t]])
            pg.barrier()
        dump_x("x2")
        if stop_after == "mixC":
            return finish()

        run_ffn(1, 2)
        dump_x("x3")
        if stop_after == "ffn2_0":
            return finish()
        run_ffn(2, 3)
        dump_x("x4")
        if stop_after == "ffn1_1":
            return finish()

        B_s3, B_o3 = Buf("s3"), Buf("o3")
        cc3 = pg.dma_ctr("cc3")
        with ExitStack() as st0:
            rmsnorm_to_hT(4, st0)
            pg.barrier()
        with ExitStack() as st:
            CW = NTOK + 2
            W1 = sb("scW1", 8 * D, BF16, st)
            W2 = sb("scW2", 8 * D, BF16, st)
            wso = sb("scWo", 8 * D, BF16, st)
            scw = sb("scw", 24, F32, st)
            w1B, w2B, woB = Buf("w1"), Buf("w2"), Buf("wo")
            w1C, w2C, woC = pg.dma_ctr("w1"), pg.dma_ctr("w2"), pg.dma_ctr("wo")
            pg.dma("sp", woC, lambda e: e.dma_start(out=scw[:], in_=scw_d.ap()), writes=[woB])
            load_w(W1, w1B, w1C, lambda k: scin_d.ap()[k * 128:(k + 1) * 128, D:2 * D], 8, D, D)
            load_w(W2, w2B, w2C, lambda k: scin_d.ap()[k * 128:(k + 1) * 128, 2 * D:3 * D], 8, D, D)
            load_w(wso, woB, woC, lambda k: scout_d.ap()[k * 128:(k + 1) * 128, :], 8, D, D)
            cvT = sb("cvT", 8 * CW, BF16, st)
            cvB = [[Buf() for _ in range(NTB)] for _ in range(8)]
            hlB = Buf("halo")
            csb = [sb(f"csb{i}", TB, F32, st) for i in range(2)]
            csB = [Buf() for _ in range(2)]
            n_ = 0
            for t in range(NTB):
                for fc in range(8):
                    pc_, pv_ = (n_ % 2), 2 + (n_ % 2)
                    s_ = n_ % 2
                    n_ += 1
                    for k in range(8):
                        pg.op("pe", lambda e, k=k, fc=fc, t=t, pc_=pc_: e.matmul(PS[pc_][:], lhsT=W1[:, k * D + fc * 128:k * D + (fc + 1) * 128], rhs=HS(k, t), start=(k == 0), stop=(k == 7)),
                              reads=[w1B, hB[k][t]], writes=[PB[pc_]], inc=(k == 7))
                    for k in range(8):
                        pg.op("pe", lambda e, k=k, fc=fc, t=t, pv_=pv_: e.matmul(PS[pv_][:], lhsT=W2[:, k * D + fc * 128:k * D + (fc + 1) * 128], rhs=HS(k, t), start=(k == 0), stop=(k == 7)),
                              reads=[w2B, hB[k][t]], writes=[PB[pv_]], inc=(k == 7))
                    pg.op("act", lambda e, pc_=pc_, s_=s_: e.activation(out=csb[s_][:], in_=PS[pc_][:], func=AF.Copy), reads=[PB[pc_]], writes=[csB[s_]])
                    pg.op("dve", lambda e, pv_=pv_, s_=s_, fc=fc, t=t: e.tensor_tensor(out=cvT[:, fc * CW + 2 + t * TB:fc * CW + 2 + (t + 1) * TB], in0=PS[pv_][:], in1=csb[s_][:], op=ALU.mult),
                          reads=[PB[pv_], csB[s_]], writes=[cvB[fc][t]])
            s3C = pg.dma_ctr("s3")
            for fc in range(8):
                pg.dma("sp", s3C, lambda e, fc=fc: e.dma_start(out=s3.ap()[:, fc * 2:fc * 2 + 2], in_=cvT[:, fc * CW + NTOK:fc * CW + NTOK + 2]),
                       reads=[cvB[fc][NTB - 1]], writes=[B_s3])
            pg.cc(cc3, lambda e: e.collective_compute("AllGather", ALU.bypass, replica_groups=PAIRS, ins=[s3.ap()], outs=[o3.ap()]),
                  reads=[B_s3], writes=[B_o3])
            halo_t = sb("halo_t", 16, BF16, st)
            hlC = pg.dma_ctr("hl")
            pg.dma("sp", hlC, lambda e: e.dma_start(out=halo_t[:], in_=o3.ap()[0:128, :]), reads=[B_o3], writes=[hlB])
            for fc in range(8):
                pg.op("dve", lambda e, fc=fc: e.tensor_scalar(out=cvT[:, fc * CW:fc * CW + 2], in0=halo_t[:, fc * 2:fc * 2 + 2], scalar1=flag[:, 0:1], scalar2=0.0, op0=ALU.mult, op1=ALU.add),
                      reads=[hlB, B_const], writes=[cvB[fc][0]])
            load_w(W1, w1B, w1C, lambda k: scin_d.ap()[k * 128:(k + 1) * 128, 0:D], 8, D, D)
            cy = csb
            cyB = csB
            gm = sb("gm", 8 * TB, BF16, st)
            gmB = [Buf() for _ in range(8)]
            n_ = 0
            for t in (1, 2, 3, 0):
                for fc in range(8):
                    pb = n_ % 2
                    s_ = n_ % 2
                    n_ += 1
                    for k in range(8):
                        pg.op("pe", lambda e, k=k, fc=fc, t=t, pb=pb: e.matmul(PS[pb][:], lhsT=W1[:, k * D + fc * 128:k * D + (fc + 1) * 128], rhs=HS(k, t), start=(k == 0), stop=(k == 7)),
                              reads=[w1B, hB[k][t]], writes=[PB[pb]], inc=(k == 7))
                    base = fc * CW + 2 + t * TB
                    rd = [cvB[fc][t]] + ([cvB[fc][t - 1]] if t > 0 else [])
                    pg.op("pool", lambda e, s_=s_, base=base, fc=fc: e.tensor_scalar(out=cy[s_][:], in0=cvT[:, base - 2:base - 2 + TB], scalar1=scw[:, 0 * 8 + fc:0 * 8 + fc + 1], scalar2=0.0, op0=ALU.mult, op1=ALU.add),
                          reads=rd + [woB], writes=[cyB[s_]])
                    pg.op("dve", lambda e, s_=s_, base=base, fc=fc: e.scalar_tensor_tensor(out=cy[s_][:], in0=cvT[:, base - 1:base - 1 + TB], scalar=scw[:, 1 * 8 + fc:1 * 8 + fc + 1], in1=cy[s_][:], op0=ALU.mult, op1=ALU.add),
                          reads=rd + [woB], writes=[cyB[s_]])
                    pg.op("dve", lambda e, s_=s_, base=base, fc=fc: e.scalar_tensor_tensor(out=cy[s_][:], in0=cvT[:, base:base + TB], scalar=scw[:, 2 * 8 + fc:2 * 8 + fc + 1], in1=cy[s_][:], op0=ALU.mult, op1=ALU.add),
                          reads=rd + [woB], writes=[cyB[s_]])
                    pg.op("dve", lambda e, s_=s_, pb=pb, fc=fc: e.tensor_tensor(out=gm[:, fc * TB:(fc + 1) * TB], in0=PS[pb][:], in1=cy[s_][:], op=ALU.mult),
                          reads=[PB[pb], cyB[s_]], writes=[gmB[fc]])
                for dc in range(8):
                    po = 2 + (dc % 2)
                    for fc in range(8):
                        pg.op("pe", lambda e, po=po, fc=fc, dc=dc: e.matmul(PS[po][:], lhsT=wso[:, fc * D + dc * 128:fc * D + (dc + 1) * 128], rhs=gm[:, fc * TB:(fc + 1) * TB], start=(fc == 0), stop=(fc == 7)),
                              reads=[woB, gmB[fc]], writes=[PB[po]], inc=(fc == 7))
                    pg.op("dve", lambda e, dc=dc, t=t, po=po: e.scalar_tensor_tensor(out=XS(dc, t), in0=PS[po][:], scalar=1.0, in1=XS(dc, t), op0=ALU.mult, op1=ALU.add),
                          reads=[PB[po]], writes=[xB[dc][t]])
            pg.barrier()
        dump_x("x5")
        if stop_after == "mix1":
            return finish()
        run_ffn(3, 5)
        dump_x("x6")
        if stop_after == "ffn2_1":
            return finish()

        with ExitStack() as st:
            ni = 6
            sq = sb("sqF", 8 * TB, BF16, st)
            rstd = sb("rstdF", TB, F32, st)
            sqB = [Buf() for _ in range(8)]
            rB = Buf()
            oring = Ring("outstg", 3, TB, F32, st)
            d_outs = []
            for t in range(NTB):
                for c in range(8):
                    if c % 2 == 0:
                        pg.op("act", lambda e, c=c, t=t: e.activation(out=sq[:, c * TB:(c + 1) * TB], in_=XS(c, t), func=AF.Square), reads=[xB[c][t]], writes=[sqB[c]])
                    else:
                        pg.op("pool", lambda e, c=c, t=t: e.tensor_tensor(out=sq[:, c * TB:(c + 1) * TB], in0=XS(c, t), in1=XS(c, t), op=ALU.mult), reads=[xB[c][t]], writes=[sqB[c]])
                for c in range(8):
                    pg.op("pe", lambda e, c=c: e.matmul(PS[7][:], lhsT=ones_bf[:], rhs=sq[:, c * TB:(c + 1) * TB], start=(c == 0), stop=(c == 7)), reads=[sqB[c], B_const], writes=[PB[7]], inc=(c == 7))
                pg.op("act", lambda e: e.activation(out=rstd[:], in_=PS[7][:], func=AF.Ln, bias=EPS), reads=[PB[7]], writes=[rB])
                pg.op("act", lambda e: e.activation(out=rstd[:], in_=rstd[:], func=AF.Exp, scale=-0.5), reads=[rB], writes=[rB])
                for c in range(8):
                    stg, stgB, stgC = oring.nxt()
                    pg.op("dve", lambda e, c=c, t=t, stg=stg: e.scalar_tensor_tensor(out=stg[:], in0=XS(c, t), scalar=norms[:, ni * 8 + c:ni * 8 + c + 1], in1=rstd[:], op0=ALU.mult, op1=ALU.mult),
                          reads=[xB[c][t], rB, B_const], writes=[stgB])
                    pg.dma("sp", stgC, lambda e, c=c, t=t, stg=stg: e.dma_start(out=out_d.ap()[c * 128:(c + 1) * 128, t * TB:(t + 1) * TB], in_=stg[:]), reads=[stgB])
            for c_ in oring.c:
                pg.streams["sp"].append(("wait", c_.sem, c_.val))
            pg.emit()
            return nc


_CACHE = {}


def _consts():
    ident = np.eye(128, dtype=np.float32)
    j = np.arange(128)[:, None]
    s = np.arange(128)[None, :]
    tri = (j >= s).astype(np.float32)
    masks = []
    t = np.arange(512)[None, :]
    for o in range(4):
        masks.append((t > 128 * o + j).astype(np.float32))
    iota = np.broadcast_to(np.arange(1, 513, dtype=np.float32)[None, :], (128, 512))
    return np.ascontiguousarray(np.concatenate([ident, tri] + masks + [iota], axis=1))


def _prep_inputs(inp):
    x = np.asarray(inp["x"], np.float32).reshape(8, NTOK, D)
    norm_list = [inp["ffn1_norm"][0], inp["mix_norm"][0], inp["ffn2_norm"][0],
                 inp["ffn1_norm"][1], inp["mix_norm"][1], inp["ffn2_norm"][1], inp["final_norm"]]
    norms = np.stack([np.asarray(n, np.float32).reshape(8, 128).T for n in norm_list], axis=1).reshape(128, 56)
    cst = _consts()
    win = np.asarray(inp["ab_w_in"][0], np.float32)
    maps = []
    for c in range(8):
        r = c % 2
        m = {"xT": np.ascontiguousarray(x[c].T), "norms": np.ascontiguousarray(norms), "cst": cst,
             "flag": np.full((128, 1), float(r), np.float32)}
        order = [(0, inp["ffn1_w_gate"], inp["ffn1_w_up"], inp["ffn1_w_down"]), (0, inp["ffn2_w_gate"], inp["ffn2_w_up"], inp["ffn2_w_down"]),
                 (1, inp["ffn1_w_gate"], inp["ffn1_w_up"], inp["ffn1_w_down"]), (1, inp["ffn2_w_gate"], inp["ffn2_w_up"], inp["ffn2_w_down"])]
        for i, (l, g, u, d) in enumerate(order):
            m[f"wg{i}"] = np.asarray(g[l], np.float32)
            m[f"wu{i}"] = np.asarray(u[l], np.float32)
            m[f"wd{i}"] = np.asarray(d[l], np.float32)

        def cols(rr):
            return np.concatenate([win[:, 256 * rr:256 * rr + 256], win[:, 512 + 256 * rr:512 + 256 * rr + 256],
                                   win[:, 1024 + 256 * rr:1024 + 256 * rr + 256], win[:, 1536 + 256 * rr:1536 + 256 * rr + 256]], axis=1)
        m["win"] = np.ascontiguousarray(np.concatenate([cols(r), cols(1 - r)], axis=1))
        m["wglu"] = np.asarray(inp["s5_w_glu"][0], np.float32)
        m["wout"] = np.asarray(inp["ab_w_out"][0], np.float32)
        m["scin"] = np.asarray(inp["sc_w_in"][0], np.float32)
        m["scw"] = np.ascontiguousarray(np.stack([np.asarray(inp["sc_conv_w"][0][k], np.float32).reshape(8, 128).T for k in range(3)], axis=1).reshape(128, 24))
        m["scout"] = np.asarray(inp["sc_w_out"][0], np.float32)
        g0 = 16 * r
        lr = np.asarray(inp["s5_lambda_re"][0][g0:g0 + 16], np.float32)
        li = np.asarray(inp["s5_lambda_im"][0][g0:g0 + 16], np.float32)
        ld = np.broadcast_to(np.asarray(inp["s5_log_dt"][0][g0:g0 + 16], np.float32)[:, None], (16, 64))

        def pl(a):
            return a.reshape(8, 2, 64).transpose(1, 2, 0).reshape(128, 8)
        z = np.zeros((128, 8), np.float32)
        m["lam"] = np.ascontiguousarray(np.concatenate([pl(lr), pl(li), pl(ld), z, z], axis=1))
        m["bre"] = np.ascontiguousarray(np.asarray(inp["s5_b_re"][0][g0:g0 + 16], np.float32))
        m["bim"] = np.ascontiguousarray(np.asarray(inp["s5_b_im"][0][g0:g0 + 16], np.float32))
        m["cre"] = np.ascontiguousarray(np.asarray(inp["s5_c_re"][0][g0:g0 + 16], np.float32))
        m["cim"] = np.ascontiguousarray(np.asarray(inp["s5_c_im"][0][g0:g0 + 16], np.float32))
        m["s5d"] = np.ascontiguousarray(np.asarray(inp["s5_d"][0][256 * r:256 * r + 256], np.float32).reshape(2, 128).T)
        maps.append(m)
    return maps


def kernel(**inputs):
    stop = os.environ.get("MK_STOP", "final")
    key = stop + os.environ.get("MK_DBG", "")
    if key not in _CACHE:
        _CACHE[key] = build(stop_after=stop)
    nc = _CACHE[key]
    maps = _prep_inputs(inputs)
    res = run_bass_kernel_spmd(nc, maps, core_ids=list(range(8)))
    _CACHE["last_res"] = res
    outT = np.stack([np.asarray(res.results[c]["outT"]) for c in range(8)], axis=0)
    out = outT.transpose(0, 2, 1).reshape(4, SEQ, D)
    return np.ascontiguousarray(out.astype(np.float32))
```

```python
import os
import math
import numpy as np
from contextlib import ExitStack
import concourse.bass as bass
import concourse.mybir as mybir
from concourse.bass_utils import run_bass_kernel_spmd

F32 = mybir.dt.float32
BF16 = mybir.dt.bfloat16
ALU = mybir.AluOpType
AF = mybir.ActivationFunctionType

D = 1024
NTOK = 2048
TB = 512
NTB = NTOK // TB
DFF = 2752
SEQ = 4096
PAIRS = [[0, 1], [2, 3], [4, 5], [6, 7]]
EPS = 1e-6
PI = math.pi
STAGES = ["ffn1_0", "proj0", "mixB", "mixC", "ffn2_0", "ffn1_1", "mix1", "ffn2_1", "final"]


class Ctr:
    def __init__(self, sem, name):
        self.sem, self.val, self.name = sem, 0, name


class Buf:
    def __init__(self, name="", excl=False):
        self.name, self.excl = name, excl
        self.lw = None
        self.rd = {}


class Prog:
    ENG = ("pe", "act", "dve", "pool", "sp")

    def __init__(self, nc, es):
        self.nc, self.es = nc, es
        self.streams = {e: [] for e in self.ENG}
        self.ctr = {e: Ctr(es.enter_context(nc.semaphore("c_" + e)), e) for e in ("pe", "act", "dve", "pool")}
        self.known = {e: {} for e in self.ENG}
        self.dma_ctrs = []
        self.dma_set = set()

    def dma_ctr(self, name):
        c = Ctr(self.es.enter_context(self.nc.semaphore("d_" + name)), name)
        self.dma_ctrs.append(c)
        self.dma_set.add(c)
        return c

    def _deps(self, eng, reads, writes):
        need = {}

        def add(cv):
            if cv is not None and need.get(cv[0], 0) < cv[1]:
                need[cv[0]] = cv[1]
        for r in reads:
            add(r.lw)
        for w in writes:
            add(w.lw)
            for c, v in w.rd.items():
                add((c, v))
        own = self.ctr.get(eng)
        for c, v in need.items():
            if c is own and eng == "pe":
                continue
            if c in self.dma_set:
                v = c.val
            if self.known[eng].get(c, 0) >= v:
                continue
            self.known[eng][c] = v
            self.streams[eng].append(("wait", c.sem, v))

    def _done(self, c, reads, writes):
        for w in writes:
            w.lw = (c, c.val)
            w.rd = {}
        for r in reads:
            if r.rd.get(c, 0) < c.val:
                r.rd[c] = c.val

    @staticmethod
    def _split(reads, writes):
        w = list(writes) + [r for r in reads if r.excl]
        ws = set(id(x) for x in w)
        r = [x for x in reads if id(x) not in ws]
        seen, w2 = set(), []
        for x in w:
            if id(x) not in seen:
                seen.add(id(x))
                w2.append(x)
        return r, w2

    def op(self, eng, fn, reads=(), writes=(), inc=True):
        reads, writes = self._split(reads, writes)
        self._deps(eng, reads, writes)
        c = self.ctr[eng]
        if inc:
            c.val += 1
            self.streams[eng].append(("op", fn, c.sem, 1))
            self._done(c, reads, writes)
        else:
            assert eng == "pe"
            self.streams[eng].append(("opn", fn))
            c.val += 1
            self._done(c, reads, writes)
            c.val -= 1

    def dma(self, q, ctr, fn, reads=(), writes=()):
        reads, writes = self._split(reads, writes)
        self._deps(q, reads, writes)
        ctr.val += 16
        self.streams[q].append(("op", fn, ctr.sem, 16))
        self._done(ctr, reads, writes)

    def cc(self, ctr, fn, reads=(), writes=()):
        reads, writes = self._split(reads, writes)
        self._deps("pool", reads, writes)
        ctr.val += 1
        self.streams["pool"].append(("op", fn, ctr.sem, 1))
        self._done(ctr, reads, writes)

    def barrier(self):
        allc = list(self.ctr.values()) + self.dma_ctrs
        for e in self.ENG:
            for c in allc:
                if c.val > self.known[e].get(c, 0):
                    self.known[e][c] = c.val
                    self.streams[e].append(("wait", c.sem, c.val))

    def emit(self):
        nc = self.nc
        _PAR.clear()
        with nc.Block() as block:
            def run(name):
                def f(e):
                    for it in self.streams[name]:
                        if it[0] == "wait":
                            e.wait_ge(it[1], it[2])
                        elif it[0] == "opn":
                            it[1](e)
                        else:
                            it[1](e).then_inc(it[2], it[3])
                return f
            block.tensor(run("pe"))
            block.scalar(run("act"))
            block.vector(run("dve"))
            block.gpsimd(run("pool"))
            block.sync(run("sp"))


_PAR = {}


def PART(e, s_):
    k = id(e)
    if k not in _PAR:
        par = e.partition_id() % 2
        _PAR[k] = (par, 1 - par)
    return _PAR[k][s_]


def build(stop_after="final", dbg=False):
    nc = bass.Bass("TRN2", target_bir_lowering=False)
    es = ExitStack()
    with es:
        def din(name, shape, dt=F32):
            return nc.dram_tensor(name, list(shape), dt, kind="ExternalInput")
        xT_d = din("xT", [D, NTOK])
        norms_d = din("norms", [128, 7 * 8])
        wg_d = [din(f"wg{i}", [D, DFF]) for i in range(4)]
        wu_d = [din(f"wu{i}", [D, DFF]) for i in range(4)]
        wd_d = [din(f"wd{i}", [DFF, D]) for i in range(4)]
        win_d = din("win", [D, 2048])
        wglu_d = din("wglu", [512, 512])
        wout_d = din("wout", [1024, D])
        scin_d = din("scin", [D, 3072])
        scw_d = din("scw", [128, 24])
        scout_d = din("scout", [D, D])
        lam_d = din("lam", [128, 40])
        bre_d = din("bre", [16, 64, 16])
        bim_d = din("bim", [16, 64, 16])
        cre_d = din("cre", [16, 16, 64])
        cim_d = din("cim", [16, 16, 64])
        s5d_d = din("s5d", [128, 2])
        cst_d = din("cst", [128, 128 + 128 + 4 * 512 + 512])
        flag_d = din("flag", [128, 1])
        out_d = nc.dram_tensor("outT", [D, NTOK], F32, kind="ExternalOutput")
        dbg_d = nc.dram_tensor("dbg2", [1024, 4096], BF16, kind="ExternalOutput") if stop_after == "mixB" else None
        s1p = [nc.dram_tensor(f"s1_{i}", [512, 2048], BF16) for i in range(4)]
        o1p = [nc.dram_tensor(f"o1_{i}", [1024, 2048], BF16) for i in range(4)]
        s2p = [nc.dram_tensor(f"s2_{i}", [256, 4096], BF16) for i in range(2)]
        o2p = [nc.dram_tensor(f"o2_{i}", [512, 4096], BF16) for i in range(2)]
        s3 = nc.dram_tensor("s3", [128, 16], BF16)
        o3 = nc.dram_tensor("o3", [256, 16], BF16)

        pg = Prog(nc, es)
        DBG = os.environ.get("MK_DBG", "") == "1"

        def dump_x(name):
            if not DBG:
                return
            dd = nc.dram_tensor("dbgx_" + name, [D, NTOK], F32, kind="ExternalOutput")
            dc_ = pg.dma_ctr("dbgx_" + name)
            for c in range(8):
                pg.dma("sp", dc_, lambda e, c=c: e.dma_start(out=dd.ap()[c * 128:(c + 1) * 128, :], in_=xT[:, c * NTOK:(c + 1) * NTOK]), reads=[b for b in xB[c]])

        def sb(name, cols, dt, st=es):
            return st.enter_context(nc.sbuf_tensor("sb_" + name, [128, cols], dt))

        xT = sb("xT", 8 * NTOK, F32)
        xB = [[Buf(f"x{c}_{t}") for t in range(NTB)] for c in range(8)]
        hT = sb("hT", 8 * NTOK, BF16)
        hB = [[Buf(f"h{c}_{t}") for t in range(NTB)] for c in range(8)]
        norms = sb("norms", 56, F32)
        cst = sb("cst", 128 + 128 + 2048 + 512, F32)
        identf = cst[:, 0:128]
        ones_bf = sb("ones_bf", 128, BF16)
        negones_bf = sb("negones_bf", 128, BF16)
        negtri_bf = sb("negtri_bf", 128, BF16)
        flag = sb("flag", 1, F32)
        B_const = Buf("const")
        d_const = pg.dma_ctr("const")
        d_x = pg.dma_ctr("x")

        def XS(c, t, w=TB):
            return xT[:, c * NTOK + t * TB: c * NTOK + t * TB + w]

        def HS(c, t, w=TB, off=0):
            return hT[:, c * NTOK + t * TB + off: c * NTOK + t * TB + off + w]

        PSW = [es.enter_context(nc.psum_tensor(f"psw{i}", [128, 1024], F32)) for i in range(4)]
        PS = [PSW[i // 2][:, (i % 2) * 512:(i % 2 + 1) * 512] for i in range(8)]
        PB = [Buf(f"ps{i}", excl=True) for i in range(8)]

        pg.dma("sp", d_const, lambda e: e.dma_start(out=norms[:], in_=norms_d.ap()), writes=[B_const])
        pg.dma("sp", d_const, lambda e: e.dma_start(out=cst[:], in_=cst_d.ap()), writes=[B_const])
        pg.dma("sp", d_const, lambda e: e.dma_start(out=flag[:], in_=flag_d.ap()), writes=[B_const])
        for c in range(8):
            for t in range(NTB):
                pg.dma("sp", d_x, lambda e, c=c, t=t: e.dma_start(out=XS(c, t), in_=xT_d.ap()[c * 128:(c + 1) * 128, t * TB:(t + 1) * TB]),
                       writes=[xB[c][t]])
        pg.op("dve", lambda e: e.memset(ones_bf[:], 1.0 / 1024.0), writes=[B_const])
        pg.op("dve", lambda e: e.memset(negones_bf[:], -1.0), writes=[B_const])
        pg.op("dve", lambda e: e.tensor_scalar(out=negtri_bf[:], in0=cst[:, 128:256], scalar1=-1.0, scalar2=0.0, op0=ALU.mult, op1=ALU.add),
              reads=[B_const], writes=[B_const])

        def norm_prep(ni, st):
            sq = [sb(f"sq{ni}_{i}", 8 * TB, BF16, st) for i in range(2)]
            rstd = [sb(f"rstd{ni}_{i}", TB, F32, st) for i in range(2)]
            sqB = [[Buf() for _ in range(8)] for _ in range(2)]
            rB = [Buf() for _ in range(2)]

            def nb(t):
                q = t % 2
                for c in range(8):
                    if c % 4 != 3:
                        pg.op("act", lambda e, c=c, t=t, q=q: e.activation(out=sq[q][:, c * TB:(c + 1) * TB], in_=XS(c, t), func=AF.Square),
                              reads=[xB[c][t]], writes=[sqB[q][c]])
                    else:
                        pg.op("pool", lambda e, c=c, t=t, q=q: e.tensor_tensor(out=sq[q][:, c * TB:(c + 1) * TB], in0=XS(c, t), in1=XS(c, t), op=ALU.mult),
                              reads=[xB[c][t]], writes=[sqB[q][c]])
                for c in range(8):
                    pg.op("pe", lambda e, c=c, q=q: e.matmul(PS[7][:], lhsT=ones_bf[:], rhs=sq[q][:, c * TB:(c + 1) * TB], start=(c == 0), stop=(c == 7)),
                          reads=[sqB[q][c], B_const], writes=[PB[7]], inc=(c == 7))
                pg.op("act", lambda e, q=q: e.activation(out=rstd[q][:], in_=PS[7][:], func=AF.Ln, bias=EPS), reads=[PB[7]], writes=[rB[q]])
                pg.op("act", lambda e, q=q: e.activation(out=rstd[q][:], in_=rstd[q][:], func=AF.Exp, scale=-0.5), reads=[rB[q]], writes=[rB[q]])
                for c in range(8):
                    pg.op("dve", lambda e, c=c, t=t, q=q: e.scalar_tensor_tensor(out=HS(c, t), in0=XS(c, t), scalar=norms[:, ni * 8 + c: ni * 8 + c + 1],
                                                                              in1=rstd[q][:], op0=ALU.mult, op1=ALU.mult),
                          reads=[xB[c][t], rB[q], B_const], writes=[hB[c][t]])
            return nb

        def rmsnorm_to_hT(ni, st):
            nb = norm_prep(ni, st)
            for t in range(NTB):
                nb(t)

        FFG = 512
        groups = []
        f0 = 0
        while f0 < DFF:
            w = min(FFG, DFF - f0)
            groups.append((f0, w))
            f0 += w

        def ffn(fi, ni, wst):
            with ExitStack() as st:
                nb = norm_prep(ni, st)
                sil = [sb(f"sil{fi}_{i}", TB, F32, st) for i in range(2)]
                silB = [Buf() for _ in range(2)]
                gt = [sb(f"g{fi}_{i}", 4 * TB, BF16, st) for i in range(2)]
                gB = [[Buf() for _ in range(4)] for _ in range(2)]
                it = 0
                gi = 0

                def load_group(gidx):
                    f0, fw = groups[gidx]
                    slot = gidx % 2
                    wgS, wuS, wdS = wst["wg"][slot], wst["wu"][slot], wst["wd"][slot]
                    dctr = wst["ctr"][slot]
                    wb = wst["B"][slot]
                    nch = (fw + 127) // 128
                    for k in range(8):
                        pg.dma("pool", dctr, lambda e, k=k, f0=f0, fw=fw, wgS=wgS: e.dma_start(out=wgS[:, k * FFG:k * FFG + fw], in_=wg_d[fi].ap()[k * 128:(k + 1) * 128, f0:f0 + fw]),
                               writes=[wb["g"][k]])
                        pg.dma("pool", dctr, lambda e, k=k, f0=f0, fw=fw, wuS=wuS: e.dma_start(out=wuS[:, k * FFG:k * FFG + fw], in_=wu_d[fi].ap()[k * 128:(k + 1) * 128, f0:f0 + fw]),
                               writes=[wb["u"][k]])
                    for ci in range(nch):
                        cw = min(128, fw - ci * 128)
                        pg.dma("pool", dctr, lambda e, ci=ci, cw=cw, f0=f0, wdS=wdS: e.dma_start(out=wdS[0:cw, ci * D:(ci + 1) * D], in_=wd_d[fi].ap()[f0 + ci * 128:f0 + ci * 128 + cw, :]),
                               writes=[wb["d"][ci]])

                load_group(0)
                nb(0)
                for gidx, (f0, fw) in enumerate(groups):
                    slot = gidx % 2
                    wgS, wuS, wdS = wst["wg"][slot], wst["wu"][slot], wst["wd"][slot]
                    wb = wst["B"][slot]
                    nch = (fw + 127) // 128
                    if gidx + 1 < len(groups):
                        load_group(gidx + 1)
                    for t in range(NTB):
                        if gidx == 0 and t + 1 < NTB:
                            nb(t + 1)
                        gs = gi % 2
                        gi += 1
                        for ci in range(nch):
                            cw = min(128, fw - ci * 128)
                            pgate, pup = (it % 2), 2 + (it % 2)
                            s_ = it % 2
                            it += 1
                            for k in range(8):
                                pg.op("pe", lambda e, k=k, ci=ci, cw=cw, t=t, pgate=pgate, wgS=wgS: e.matmul(PS[pgate][0:cw, :], lhsT=wgS[:, k * FFG + ci * 128:k * FFG + ci * 128 + cw], rhs=HS(k, t), start=(k == 0), stop=(k == 7)),
                                      reads=[wb["g"][k], hB[k][t]], writes=[PB[pgate]], inc=(k == 7))
                            for k in range(8):
                                pg.op("pe", lambda e, k=k, ci=ci, cw=cw, t=t, pup=pup, wuS=wuS: e.matmul(PS[pup][0:cw, :], lhsT=wuS[:, k * FFG + ci * 128:k * FFG + ci * 128 + cw], rhs=HS(k, t), start=(k == 0), stop=(k == 7)),
                                      reads=[wb["u"][k], hB[k][t]], writes=[PB[pup]], inc=(k == 7))
                            pg.op("act", lambda e, cw=cw, pgate=pgate, s_=s_: e.activation(out=sil[s_][0:cw, :], in_=PS[pgate][0:cw, :], func=AF.Silu),
                                  reads=[PB[pgate]], writes=[silB[s_]])
                            pg.op("dve", lambda e, cw=cw, pup=pup, s_=s_, gs=gs, ci=ci: e.tensor_tensor(out=gt[gs][0:cw, ci * TB:(ci + 1) * TB], in0=PS[pup][0:cw, :], in1=sil[s_][0:cw, :], op=ALU.mult),
                                  reads=[PB[pup], silB[s_]], writes=[gB[gs][ci]])
                        for dc in range(8):
                            po = 4 + (dc % 2)
                            for ci in range(nch):
                                cw = min(128, fw - ci * 128)
                                pg.op("pe", lambda e, dc=dc, ci=ci, cw=cw, po=po, gs=gs, wdS=wdS, nch=nch: e.matmul(PS[po][:], lhsT=wdS[0:cw, ci * D + dc * 128:ci * D + (dc + 1) * 128], rhs=gt[gs][0:cw, ci * TB:(ci + 1) * TB], start=(ci == 0), stop=(ci == nch - 1)),
                                      reads=[wb["d"][ci], gB[gs][ci]], writes=[PB[po]], inc=(ci == nch - 1))
                            pg.op("dve", lambda e, dc=dc, t=t, po=po: e.scalar_tensor_tensor(out=XS(dc, t), in0=PS[po][:], scalar=0.5, in1=XS(dc, t), op0=ALU.mult, op1=ALU.add),
                                  reads=[PB[po]], writes=[xB[dc][t]])
                pg.barrier()

        def finish(dump_x=True):
            d_out = pg.dma_ctr("out")
            for c in range(8):
                pg.dma("sp", d_out, lambda e, c=c: e.dma_start(out=out_d.ap()[c * 128:(c + 1) * 128, :], in_=xT[:, c * NTOK:(c + 1) * NTOK]),
                       reads=[b for b in xB[c]])
            pg.streams["sp"].append(("wait", d_out.sem, d_out.val))
            if dbg:
                pass
            pg.emit()
            return nc


        def mk_wst(tag, st):
            return {"n": 0,
                    "wg": [sb(f"wgS{tag}{i}", 8 * FFG, BF16, st) for i in range(2)],
                    "wu": [sb(f"wuS{tag}{i}", 8 * FFG, BF16, st) for i in range(2)],
                    "wd": [sb(f"wdS{tag}{i}", 4 * D, BF16, st) for i in range(2)],
                    "B": [{"g": [Buf() for _ in range(8)], "u": [Buf() for _ in range(8)], "d": [Buf() for _ in range(4)]} for i in range(2)],
                    "ctr": [pg.dma_ctr(f"w{tag}{i}") for i in range(2)]}

        def run_ffn(fi, ni):
            with ExitStack() as st:
                ffn(fi, ni, mk_wst(fi, st))

        class Ring:
            def __init__(self, name, n, cols, dt, st):
                self.t = [sb(f"{name}{i}", cols, dt, st) for i in range(n)]
                self.b = [Buf(f"{name}{i}") for i in range(n)]
                self.c = [pg.dma_ctr(f"{name}{i}") for i in range(n)]
                self.i = 0

            def nxt(self):
                k = self.i % len(self.t)
                self.i += 1
                return self.t[k], self.b[k], self.c[k]

        def load_w(dst, dstB, ctr, src_ap_fn, nk, ncols, stride):
            for k in range(nk):
                pg.dma("pool", ctr, lambda e, k=k: e.dma_start(out=dst[:, k * stride:k * stride + ncols], in_=src_ap_fn(k)), writes=[dstB])

        run_ffn(0, 0)
        if stop_after == "ffn1_0":
            return finish()

        B_s1 = [Buf(f"s1_{i}") for i in range(64)]
        B_o1 = Buf("o1")
        cc1 = [pg.dma_ctr(f"cc1_{i}") for i in range(4)]
        with ExitStack() as st:
            nb1 = norm_prep(1, st)
            wi = sb("wi", 8 * 2048, BF16, st)
            wiB = Buf("wi")
            wic = pg.dma_ctr("wi")
            load_w(wi, wiB, wic, lambda k: win_d.ap()[k * 128:(k + 1) * 128, :], 8, 2048, 2048)
            nb1(0)
            ring = Ring("stg1", 3, TB, BF16, st)
            nb = 0
            s1v = s1p[3].ap().rearrange("r (t f) -> (r t) f", f=512)
            for t in range(NTB):
                if t + 1 < NTB:
                    nb1(t + 1)
                for part in range(2):
                    for j in range(6):
                        col0 = part * 1024 + j * 128
                        pb = nb % 2
                        for k in range(8):
                            pg.op("pe", lambda e, k=k, t=t, col0=col0, pb=pb: e.matmul(PS[pb][:], lhsT=wi[:, k * 2048 + col0:k * 2048 + col0 + 128], rhs=HS(k, t), start=(k == 0), stop=(k == 7)),
                                  reads=[wiB, hB[k][t]], writes=[PB[pb]], inc=(k == 7))
                        stg, stgB, stgC = ring.nxt()
                        scale = 0.125 if j in (2, 3) else 1.0
                        if nb % 2 == 0:
                            pg.op("act", lambda e, stg=stg, pb=pb, scale=scale: e.activation(out=stg[:], in_=PS[pb][:], func=AF.Copy, scale=scale), reads=[PB[pb]], writes=[stgB])
                        else:
                            pg.op("dve", lambda e, stg=stg, pb=pb, scale=scale: e.tensor_scalar(out=stg[:], in0=PS[pb][:], scalar1=scale, scalar2=0.0, op0=ALU.mult, op1=ALU.add), reads=[PB[pb]], writes=[stgB])
                        row0 = part * 256 + (j % 2) * 128
                        pg.dma("sp", stgC, lambda e, stg=stg, row0=row0, t=t, j=j: e.dma_start(out=s1p[j // 2].ap()[row0:row0 + 128, t * TB:(t + 1) * TB], in_=stg[:]),
                               reads=[stgB], writes=[B_s1[nb % 64]])
                        nb += 1
            for tt in range(16):
                pb = 2 + (tt % 2)
                for half in range(2):
                    c0 = half * 1024 + 768
                    for k in range(8):
                        pg.op("pe", lambda e, k=k, tt=tt, c0=c0, pb=pb, half=half: e.matmul(PS[pb][:, half * 256:(half + 1) * 256], lhsT=hT[:, k * NTOK + tt * 128:k * NTOK + (tt + 1) * 128], rhs=wi[:, k * 2048 + c0:k * 2048 + c0 + 256], start=(k == 0), stop=(k == 7)),
                              reads=[wiB, hB[k][tt // 4]], writes=[PB[pb]], inc=(k == 7))
                stg, stgB, stgC = ring.nxt()
                pg.op("act", lambda e, stg=stg, pb=pb: e.activation(out=stg[:], in_=PS[pb][:], func=AF.Copy), reads=[PB[pb]], writes=[stgB])
                pg.dma("sp", stgC, lambda e, stg=stg, tt=tt: e.dma_start(out=s1v[tt * 128:(tt + 1) * 128, :], in_=stg[:]), reads=[stgB], writes=[B_s1[nb % 64]])
                nb += 1
            for i in range(4):
                pg.cc(cc1[i], lambda e, i=i: e.collective_compute("AllGather", ALU.bypass, replica_groups=PAIRS, ins=[s1p[i].ap()], outs=[o1p[i].ap()]),
                      reads=B_s1, writes=[B_o1])
            pg.barrier()
        if stop_after == "proj0":
            return finish()

        B_s2 = [Buf(f"s2_{i}") for i in range(64)]
        B_o2 = Buf("o2")
        cc2 = [pg.dma_ctr(f"cc2_{i}") for i in range(2)]
        ns2 = [0]
        with ExitStack() as st:
            QT = hT[:, 0:2 * SEQ]
            KT = hT[:, 2 * SEQ:4 * SEQ]
            VT = sb("VT", 32 * 256, BF16, st)
            qkvB = Buf("qkv")
            qkvL = []

            def qkv_fresh():
                b_ = Buf()
                qkvL.append(b_)
                return b_
            qkvC = pg.dma_ctr("qkv")
            maskb = sb("maskb", 4 * 512, BF16, st)
            maskB = Buf("mask")
            pg.op("dve", lambda e: e.tensor_copy(out=maskb[:], in_=cst[:, 256:256 + 2048]), reads=[B_const], writes=[maskB])
            for s_ in range(2):
                o1v = o1p[3].ap()[s_ * 512:(s_ + 1) * 512, :].rearrange("r (t f) -> (r t) f", f=512)
                for j, dst in ((2, QT), (3, QT), (4, KT), (5, KT)):
                    cidx = j % 2
                    o1f = o1p[j // 2].ap()[s_ * 512:(s_ + 1) * 512, :].rearrange("(a q) t -> a q t", a=2)
                    pg.dma("pool", qkvC, lambda e, s_=s_, j=j, dst=dst, cidx=cidx, o1f=o1f: e.dma_start(
                        out=dst[:, cidx * SEQ + s_ * NTOK:cidx * SEQ + (s_ + 1) * NTOK],
                        in_=o1f[bass.ds(PART(e, s_), 1), cidx * 128:(cidx + 1) * 128, :].rearrange("a p t -> (a p) t")),
                        reads=[B_o1], writes=[qkv_fresh()])
                pg.dma("pool", qkvC, lambda e, s_=s_, o1v=o1v: e.dma_start(
                    out=VT[:, s_ * 4096:(s_ + 1) * 4096].rearrange("p (n f) -> p n f", f=256),
                    in_=o1v.rearrange("(n p) f -> p n f", p=128)[:, :, bass.ds(PART(e, s_) * 256, 256)]),
                    reads=[B_o1], writes=[qkv_fresh()])
            mask2 = sb("mask2", 4 * 1024, BF16, st)
            for o in range(4):
                for hh in range(2):
                    pg.op("dve", lambda e, o=o, hh=hh: e.tensor_copy(out=mask2[:, o * 1024 + hh * 512:o * 1024 + (hh + 1) * 512], in_=maskb[:, o * 512:(o + 1) * 512]),
                          reads=[maskB], writes=[maskB])
            W2 = 2 * TB
            Et = [sb(f"E{i}", W2, F32, st) for i in range(2)]
            St = [[sb(f"S{i}_{k}", W2, BF16, st) for k in range(2)] for i in range(2)]
            Wt = [[sb(f"W{i}_{k}", W2, BF16, st) for k in range(2)] for i in range(2)]
            At = [sb(f"A{i}", W2, BF16, st) for i in range(2)]
            EB = [Buf() for _ in range(2)]
            SB_ = [[Buf() for _ in range(2)] for _ in range(2)]
            WB = [[Buf() for _ in range(2)] for _ in range(2)]
            AB = [Buf() for _ in range(2)]
            oring = Ring("ostg", 3, TB, BF16, st)
            steps = []
            for qb in range(8):
                kbs = list(range(4 * qb + 3, -1, -1))
                for idx, kb in enumerate(kbs):
                    steps.append((qb, kb, idx == 0, kb == 0))

            def zmm(i, p):
                qb, kb, first, last = steps[i]
                for hl in range(2):
                    r0 = hl * 64
                    pg.op("pe", lambda e, p=p, hl=hl, r0=r0, kb=kb, qb=qb: e.matmul(PS[2 * p + hl], lhsT=KT[r0:r0 + 64, p * SEQ + kb * 128:p * SEQ + (kb + 1) * 128],
                                                                                 rhs=QT[r0:r0 + 64, p * SEQ + qb * TB:p * SEQ + (qb + 1) * TB], start=True, stop=False),
                          reads=[qkvL[-1]], writes=[PB[2 * p + hl]])

            def es_ops(i, p):
                qb, kb, first, last = steps[i]
                k = i % 2
                pg.op("act", lambda e, p=p: e.activation(out=Et[p][:], in_=PSW[p][:], func=AF.Exp), reads=[PB[2 * p], PB[2 * p + 1]], writes=[EB[p]])
                pg.op("act", lambda e, p=p, k=k: e.activation(out=St[p][k][:], in_=Et[p][:], func=AF.Ln, bias=1.0), reads=[EB[p]], writes=[SB_[p][k]])
                o = kb - 4 * qb
                if o >= 0:
                    pg.op("dve", lambda e, p=p, k=k, o=o: e.tensor_tensor(out=St[p][k][:], in0=St[p][k][:], in1=mask2[:, o * 1024:(o + 1) * 1024], op=ALU.mult),
                          reads=[maskB], writes=[SB_[p][k]])

            def tri_ops(i, p):
                qb, kb, first, last = steps[i]
                k = i % 2
                for hl in range(2):
                    hs = slice(hl * TB, (hl + 1) * TB)
                    pg.op("pe", lambda e, p=p, hl=hl, k=k, hs=hs, first=first: e.matmul(PS[2 * p + hl], lhsT=negtri_bf[:], rhs=St[p][k][:, hs], start=False, stop=first),
                          reads=[B_const, SB_[p][k]], writes=[PB[2 * p + hl]], inc=first)
                    if not first:
                        pg.op("pe", lambda e, p=p, hl=hl, hs=hs: e.matmul(PS[2 * p + hl], lhsT=negones_bf[:], rhs=At[p][:, hs], start=False, stop=True),
                              reads=[B_const, AB[p]], writes=[PB[2 * p + hl]])

            def w_ops(i, p):
                qb, kb, first, last = steps[i]
                k = i % 2
                pg.op("act", lambda e, p=p, k=k: e.activation(out=Wt[p][k][:], in_=PSW[p][:], func=AF.Exp), reads=[PB[2 * p], PB[2 * p + 1]], writes=[WB[p][k]])
                o = kb - 4 * qb
                if o >= 0:
                    pg.op("dve", lambda e, p=p, k=k, o=o: e.tensor_tensor(out=Wt[p][k][:], in0=Wt[p][k][:], in1=mask2[:, o * 1024:(o + 1) * 1024], op=ALU.mult),
                          reads=[maskB], writes=[WB[p][k]])

            def pv_ops(i, p):
                qb, kb, first, last = steps[i]
                k = i % 2
                ob = 4 + p + 2 * (qb % 2)
                for hl in range(2):
                    h = 2 * p + hl
                    hs = slice(hl * TB, (hl + 1) * TB)
                    pg.op("pe", lambda e, ob=ob, hl=hl, h=h, kb=kb, p=p, k=k, hs=hs, first=first, last=last: e.matmul(PS[ob][hl * 64:(hl + 1) * 64, :], lhsT=VT[:, kb * 256 + h * 64:kb * 256 + (h + 1) * 64], rhs=Wt[p][k][:, hs], start=first, stop=last),
                          reads=[qkvL[-1], WB[p][k]], writes=[PB[ob]])
                if not last:
                    eng = "dve" if p == 0 else "pool"
                    if first:
                        pg.op(eng, lambda e, p=p, k=k: e.tensor_copy(out=At[p][:], in_=St[p][k][:]), reads=[SB_[p][k]], writes=[AB[p]])
                    else:
                        pg.op(eng, lambda e, p=p, k=k: e.tensor_tensor(out=At[p][:], in0=At[p][:], in1=St[p][k][:], op=ALU.add), reads=[SB_[p][k]], writes=[AB[p]])
                else:
                    stg, stgB, stgC = oring.nxt()
                    pg.op("dve", lambda e, stg=stg, ob=ob: e.tensor_copy(out=stg[:], in_=PS[ob]), reads=[PB[ob]], writes=[stgB])
                    pg.dma("sp", stgC, lambda e, stg=stg, qb=qb, p=p: e.dma_start(out=s2p[1].ap()[p * 128:(p + 1) * 128, qb * TB:(qb + 1) * TB], in_=stg[:]),
                           reads=[stgB], writes=[B_s2[ns2[0] % 64]])
                    ns2[0] += 1

            n_st = len(steps)
            for p in range(2):
                zmm(0, p)
            for p in range(2):
                es_ops(0, p)
            for i in range(n_st):
                for p in range(2):
                    tri_ops(i, p)
                for p in range(2):
                    w_ops(i, p)
                for p in range(2):
                    if i + 1 < n_st:
                        zmm(i + 1, p)
                    pv_ops(i, p)
                if i + 1 < n_st:
                    for p in range(2):
                        es_ops(i + 1, p)
            pg.barrier()

        with ExitStack() as st:
            UT = hT[:, 0:2 * SEQ]
            uB = Buf("u")
            uC = pg.dma_ctr("u")
            for s_ in range(2):
                o1f = o1p[0].ap()[s_ * 512:(s_ + 1) * 512, :].rearrange("(a q) t -> a q t", a=2)
                for j in (0, 1):
                    pg.dma("sp", uC, lambda e, s_=s_, j=j, o1f=o1f: e.dma_start(
                        out=UT[:, j * SEQ + s_ * NTOK:j * SEQ + (s_ + 1) * NTOK],
                        in_=o1f[bass.ds(PART(e, s_), 1), j * 128:(j + 1) * 128, :].rearrange("a p t -> (a p) t")),
                        reads=[B_o1], writes=[uB])
            L = sb("lam", 40, F32, st)
            s5d = sb("s5d", 2, F32, st)
            pcB = Buf("s5pre")
            pcC = pg.dma_ctr("s5pre")
            pg.dma("sp", pcC, lambda e: e.dma_start(out=L[:], in_=lam_d.ap()), writes=[pcB])
            pg.dma("sp", pcC, lambda e: e.dma_start(out=s5d[:], in_=s5d_d.ap()), writes=[pcB])
            sc = sb("s5sc", 16 * 8, F32, st)

            def SC(i):
                return sc[:, i * 8:(i + 1) * 8]
            lr, li, ld = L[:, 0:8], L[:, 8:16], L[:, 16:24]
            DT, MAG, TH, SN1, CS1, ABR, ABI, RDEN, NR, CRE, CIM, T0, T1 = [SC(i) for i in range(13)]

            def dv(fn, **kw):
                pg.op("dve", fn, reads=[pcB], writes=[pcB])

            def ac(fn):
                pg.op("act", fn, reads=[pcB], writes=[pcB])
            ac(lambda e: e.activation(out=DT, in_=ld, func=AF.Exp))
            dv(lambda e: e.tensor_tensor(out=T0, in0=lr, in1=DT, op=ALU.mult))
            ac(lambda e: e.activation(out=MAG, in_=T0, func=AF.Exp))
            dv(lambda e: e.tensor_tensor(out=TH, in0=li, in1=DT, op=ALU.mult))
            MAGIC = 12582912.0

            def range_sin(dst, x, tmp):
                dv(lambda e: e.tensor_scalar(out=tmp, in0=x, scalar1=1.0 / (2 * PI), scalar2=MAGIC, op0=ALU.mult, op1=ALU.add))
                dv(lambda e: e.tensor_scalar(out=tmp, in0=tmp, scalar1=-MAGIC, scalar2=0.0, op0=ALU.add, op1=ALU.add))
                dv(lambda e: e.scalar_tensor_tensor(out=x, in0=tmp, scalar=-2 * PI, in1=x, op0=ALU.mult, op1=ALU.add))
                dv(lambda e: e.tensor_scalar(out=x, in0=x, scalar1=-PI, scalar2=PI, op0=ALU.max, op1=ALU.min))
                ac(lambda e: e.activation(out=dst, in_=x, func=AF.Sin))
            T2 = SC(13)
            dv(lambda e: e.tensor_scalar(out=T0, in0=TH, scalar1=1.0, scalar2=0.0, op0=ALU.mult, op1=ALU.add))
            range_sin(SN1, T0, T2)
            dv(lambda e: e.tensor_scalar(out=T0, in0=TH, scalar1=1.0, scalar2=0.5 * PI, op0=ALU.mult, op1=ALU.add))
            range_sin(CS1, T0, T2)
            dv(lambda e: e.tensor_tensor(out=ABR, in0=MAG, in1=CS1, op=ALU.mult))
            dv(lambda e: e.tensor_tensor(out=ABI, in0=MAG, in1=SN1, op=ALU.mult))
            dv(lambda e: e.tensor_tensor(out=T0, in0=lr, in1=lr, op=ALU.mult))
            dv(lambda e: e.tensor_tensor(out=T1, in0=li, in1=li, op=ALU.mult))
            dv(lambda e: e.tensor_tensor(out=T0, in0=T0, in1=T1, op=ALU.add))
            dv(lambda e: e.reciprocal(out=RDEN, in_=T0))
            dv(lambda e: e.tensor_scalar(out=NR, in0=ABR, scalar1=-1.0, scalar2=0.0, op0=ALU.add, op1=ALU.add))
            dv(lambda e: e.tensor_tensor(out=T0, in0=NR, in1=lr, op=ALU.mult))
            dv(lambda e: e.tensor_tensor(out=T1, in0=ABI, in1=li, op=ALU.mult))
            dv(lambda e: e.tensor_tensor(out=T0, in0=T0, in1=T1, op=ALU.add))
            dv(lambda e: e.tensor_tensor(out=CRE, in0=T0, in1=RDEN, op=ALU.mult))
            dv(lambda e: e.tensor_tensor(out=T0, in0=ABI, in1=lr, op=ALU.mult))
            dv(lambda e: e.tensor_tensor(out=T1, in0=NR, in1=li, op=ALU.mult))
            dv(lambda e: e.tensor_tensor(out=T0, in0=T0, in1=T1, op=ALU.subtract))
            dv(lambda e: e.tensor_tensor(out=CIM, in0=T0, in1=RDEN, op=ALU.mult))
            CSt = sb("CSt", 8 * TB, F32, st)
            SNt = sb("SNt", 8 * TB, F32, st)
            tmpA = sb("tmpA", TB, F32, st)
            tmpA2 = sb("tmpA2", TB, F32, st)
            iota = cst[:, 256 + 2048:256 + 2048 + 512]
            for j in range(8):
                for tab, ph in ((SNt, 0.0), (CSt, 0.5 * PI)):
                    dv(lambda e, j=j, ph=ph: e.tensor_scalar(out=tmpA[:], in0=iota, scalar1=TH[:, j:j + 1], scalar2=ph, op0=ALU.mult, op1=ALU.add))
                    range_sin(tab[:, j * TB:(j + 1) * TB], tmpA[:], tmpA2[:])
            BDr = sb("BDr", 8 * 128, BF16, st)
            BDi = sb("BDi", 8 * 128, BF16, st)
            CTr = sb("CTr", 8 * 128, BF16, st)
            CTi = sb("CTi", 8 * 128, BF16, st)
            st2 = ExitStack()
            BRr = sb("BRr", 8 * 128, F32, st2)
            BRi = sb("BRi", 8 * 128, F32, st2)
            CRr = sb("CRr", 8 * 128, F32, st2)
            CRi = sb("CRi", 8 * 128, F32, st2)
            bcB = Buf("bc")
            bcC = pg.dma_ctr("bc")

            bc_list = []

            def fresh():
                b_ = Buf()
                bc_list.append(b_)
                return b_
            for tl in (BRr, BRi, CRr, CRi):
                pg.op("pool", lambda e, tl=tl: e.memset(tl[:], 0.0), writes=[bcB])
            for j in range(8):
                jj = j % 4
                for g in range(2):
                    for tl, src in ((BRr, bre_d), (BRi, bim_d)):
                        pg.dma("sp", bcC, lambda e, tl=tl, src=src, j=j, jj=jj, g=g: e.dma_start(out=tl[64 * g:64 * g + 64, j * 128 + 32 * jj + 16 * g:j * 128 + 32 * jj + 16 * g + 16], in_=src.ap()[2 * j + g, :, :]),
                               reads=[bcB], writes=[fresh()])
                    for tl, src in ((CRr, cre_d), (CRi, cim_d)):
                        pg.dma("sp", bcC, lambda e, tl=tl, src=src, j=j, jj=jj, g=g: e.dma_start(out=tl[32 * jj + 16 * g:32 * jj + 16 * g + 16, j * 128 + 64 * g:j * 128 + 64 * g + 64], in_=src.ap()[2 * j + g, :, :]),
                               reads=[bcB], writes=[fresh()])
            bbr = sb("bbr", 128, F32, st2)
            bbi = sb("bbi", 128, F32, st2)
            tmpB = sb("tmpB", 128, F32, st2)
            bcLast = bc_list[-1]
            bbB = Buf("bb")
            matB = Buf("mats")
            for j in range(8):
                sl = slice(j * 128, (j + 1) * 128)
                pg.op("dve", lambda e, sl=sl, j=j: e.tensor_scalar(out=tmpB[:], in0=BRi[:, sl], scalar1=CIM[:, j:j + 1], scalar2=0.0, op0=ALU.mult, op1=ALU.add), reads=[bcB, bcLast, pcB], writes=[bbB])
                pg.op("dve", lambda e, sl=sl, j=j: e.scalar_tensor_tensor(out=bbr[:], in0=BRr[:, sl], scalar=CRE[:, j:j + 1], in1=tmpB[:], op0=ALU.mult, op1=ALU.subtract), reads=[bcB, bcLast, pcB], writes=[bbB])
                pg.op("dve", lambda e, sl=sl, j=j: e.tensor_scalar(out=tmpB[:], in0=BRr[:, sl], scalar1=CIM[:, j:j + 1], scalar2=0.0, op0=ALU.mult, op1=ALU.add), reads=[bcB, bcLast, pcB], writes=[bbB])
                pg.op("dve", lambda e, sl=sl, j=j: e.scalar_tensor_tensor(out=bbi[:], in0=BRi[:, sl], scalar=CRE[:, j:j + 1], in1=tmpB[:], op0=ALU.mult, op1=ALU.add), reads=[bcB, bcLast, pcB], writes=[bbB])
                for src, dst, neg, sliced in ((bbr, BDr, False, False), (bbi, BDi, False, False), (CRr, CTr, False, True), (CRi, CTi, True, True)):
                    srcap = (lambda src=src, sl=sl, sliced=sliced: src[:, sl] if sliced else src[:])
                    pg.op("pe", lambda e, srcap=srcap: e.transpose(out=PS[6][:, 0:128], in_=srcap(), identity=identf), reads=[bbB, bcB, bcLast, B_const], writes=[PB[6]])
                    pg.op("act", lambda e, dst=dst, sl=sl, neg=neg: e.activation(out=dst[:, sl], in_=PS[6][:, 0:128], func=AF.Copy, scale=(-1.0 if neg else 1.0)), reads=[PB[6]], writes=[matB])
            pg.barrier()
            st2.close()
            hrp = sb("hrp", 8, F32, st)
            hip = sb("hip", 8, F32, st)
            hpB = [Buf() for _ in range(8)]
            hpBi = [Buf() for _ in range(8)]
            pg.op("dve", lambda e: e.memset(hrp[:], 0.0), writes=hpB)
            pg.op("dve", lambda e: e.memset(hip[:], 0.0), writes=hpBi)
            nbuf = 3
            t1 = [sb(f"t1_{i}", TB, F32, st) for i in range(nbuf)]
            t2 = [sb(f"t2_{i}", TB, F32, st) for i in range(nbuf)]
            t3 = [sb(f"t3_{i}", TB, F32, st) for i in range(nbuf)]
            t4 = [sb(f"t4_{i}", TB, F32, st) for i in range(nbuf)]
            gr = [sb(f"gr_{i}", TB, F32, st) for i in range(nbuf)]
            gi_ = [sb(f"gi_{i}", TB, F32, st) for i in range(nbuf)]
            hr = [sb(f"hr_{i}", TB, BF16, st) for i in range(nbuf)]
            hi = [sb(f"hi_{i}", TB, BF16, st) for i in range(nbuf)]
            tc_ = [sb(f"tcar{i}", 4, F32, st) for i in range(nbuf)]
            wB_ = [[Buf() for _ in range(12)] for _ in range(nbuf)]
            ysb = sb("ysb", TB, F32, st)
            y2 = sb("y2", TB, F32, st)
            yth = sb("yth", TB, F32, st)
            yB = Buf("y")
            zring = Ring("zstg", 2, TB, BF16, st)
            units = [(c, k, jj) for c in range(2) for k in range(8) for jj in range(4)]

            def uvars(u):
                c, k, jj = units[u]
                j = 4 * c + jj
                b = u % nbuf
                px, pi_ = (0, 1) if u % 2 == 0 else (2, 3)
                return dict(c=c, k=k, jj=jj, j=j, b=b, px=px, pi_=pi_, W_=wB_[b], sl=slice(j * 128, (j + 1) * 128), tsl=slice(j * TB, (j + 1) * TB),
                            usl=slice(c * SEQ + k * TB, c * SEQ + (k + 1) * TB), yb_=4 + (k % 2))

            def stage1(u):
                v = uvars(u)
                c, k, jj, j, b, px, pi_, W_, sl, tsl, usl, yb_ = (v[x] for x in ("c", "k", "jj", "j", "b", "px", "pi_", "W_", "sl", "tsl", "usl", "yb_"))
                pg.op("pe", lambda e, px=px, sl=sl, usl=usl: e.matmul(PS[px], lhsT=BDr[:, sl], rhs=UT[:, usl], start=True, stop=True), reads=[matB, uB], writes=[PB[px]])
                pg.op("pe", lambda e, pi_=pi_, sl=sl, usl=usl: e.matmul(PS[pi_], lhsT=BDi[:, sl], rhs=UT[:, usl], start=True, stop=True), reads=[matB, uB], writes=[PB[pi_]])
                pg.op("dve", lambda e, b=b, tsl=tsl, px=px: e.tensor_tensor(out=t1[b][:], in0=PS[px], in1=CSt[:, tsl], op=ALU.mult), reads=[PB[px], pcB], writes=[W_[2]])
                pg.op("dve", lambda e, b=b, tsl=tsl, pi_=pi_: e.tensor_tensor(out=t2[b][:], in0=PS[pi_], in1=SNt[:, tsl], op=ALU.mult), reads=[PB[pi_], pcB], writes=[W_[3]])
                pg.op("dve", lambda e, b=b, tsl=tsl, pi_=pi_: e.tensor_tensor(out=t3[b][:], in0=PS[pi_], in1=CSt[:, tsl], op=ALU.mult), reads=[PB[pi_], pcB], writes=[W_[4]])
                pg.op("dve", lambda e, b=b, tsl=tsl, px=px: e.tensor_tensor(out=t4[b][:], in0=PS[px], in1=SNt[:, tsl], op=ALU.mult), reads=[PB[px], pcB], writes=[W_[5]])
                pg.op("pool", lambda e, b=b: e.tensor_tensor(out=t1[b][:], in0=t1[b][:], in1=t2[b][:], op=ALU.add), reads=[W_[3]], writes=[W_[2]])
                pg.op("pool", lambda e, b=b: e.tensor_tensor(out=t3[b][:], in0=t3[b][:], in1=t4[b][:], op=ALU.subtract), reads=[W_[5]], writes=[W_[4]])

            def stage2(u):
                v = uvars(u)
                c, k, jj, j, b, px, pi_, W_, sl, tsl, usl, yb_ = (v[x] for x in ("c", "k", "jj", "j", "b", "px", "pi_", "W_", "sl", "tsl", "usl", "yb_"))
                pg.op("dve", lambda e, b=b, j=j: e.tensor_tensor_scan(out=gr[b][:], data0=MAG[:, j:j + 1].to_broadcast([128, TB]), data1=t1[b][:], initial=hrp[:, j:j + 1], op0=ALU.mult, op1=ALU.add),
                      reads=[W_[2], pcB, hpB[j]], writes=[W_[6]])
                pg.op("dve", lambda e, b=b, j=j: e.tensor_tensor_scan(out=gi_[b][:], data0=MAG[:, j:j + 1].to_broadcast([128, TB]), data1=t3[b][:], initial=hip[:, j:j + 1], op0=ALU.mult, op1=ALU.add),
                      reads=[W_[4], pcB, hpB[j]], writes=[W_[7]])
                cl = j * TB + TB - 1
                tcb = tc_[b]
                pg.op("dve", lambda e, b=b, cl=cl, tcb=tcb: e.tensor_tensor(out=tcb[:, 0:1], in0=gi_[b][:, TB - 1:TB], in1=SNt[:, cl:cl + 1], op=ALU.mult), reads=[W_[7], pcB], writes=[W_[10]])
                pg.op("dve", lambda e, b=b, cl=cl, tcb=tcb: e.tensor_tensor(out=tcb[:, 1:2], in0=gr[b][:, TB - 1:TB], in1=CSt[:, cl:cl + 1], op=ALU.mult), reads=[W_[6], pcB], writes=[W_[10]])
                pg.op("dve", lambda e, j=j, tcb=tcb: e.tensor_tensor(out=hrp[:, j:j + 1], in0=tcb[:, 1:2], in1=tcb[:, 0:1], op=ALU.subtract), reads=[W_[10]], writes=[hpB[j]])
                pg.op("pool", lambda e, b=b, cl=cl, tcb=tcb: e.tensor_tensor(out=tcb[:, 2:3], in0=gr[b][:, TB - 1:TB], in1=SNt[:, cl:cl + 1], op=ALU.mult), reads=[W_[6], pcB], writes=[W_[11]])
                pg.op("pool", lambda e, b=b, cl=cl, tcb=tcb: e.tensor_tensor(out=tcb[:, 3:4], in0=gi_[b][:, TB - 1:TB], in1=CSt[:, cl:cl + 1], op=ALU.mult), reads=[W_[7], pcB], writes=[W_[11]])
                pg.op("pool", lambda e, j=j, tcb=tcb: e.tensor_tensor(out=hip[:, j:j + 1], in0=tcb[:, 3:4], in1=tcb[:, 2:3], op=ALU.add), reads=[W_[11]], writes=[hpBi[j]])
                pg.op("dve", lambda e, b=b, tsl=tsl: e.tensor_tensor(out=t1[b][:], in0=gr[b][:], in1=CSt[:, tsl], op=ALU.mult), reads=[W_[6], pcB], writes=[W_[2]])
                pg.op("dve", lambda e, b=b, tsl=tsl: e.tensor_tensor(out=t2[b][:], in0=gi_[b][:], in1=SNt[:, tsl], op=ALU.mult), reads=[W_[7], pcB], writes=[W_[3]])
                pg.op("dve", lambda e, b=b: e.tensor_tensor(out=hr[b][:], in0=t1[b][:], in1=t2[b][:], op=ALU.subtract), reads=[W_[2], W_[3]], writes=[W_[8]])
                pg.op("pool", lambda e, b=b, tsl=tsl: e.tensor_tensor(out=t3[b][:], in0=gi_[b][:], in1=CSt[:, tsl], op=ALU.mult), reads=[W_[7], pcB], writes=[W_[4]])
                pg.op("pool", lambda e, b=b, tsl=tsl: e.tensor_tensor(out=t4[b][:], in0=gr[b][:], in1=SNt[:, tsl], op=ALU.mult), reads=[W_[6], pcB], writes=[W_[5]])
                pg.op("pool", lambda e, b=b: e.tensor_tensor(out=hi[b][:], in0=t3[b][:], in1=t4[b][:], op=ALU.add), reads=[W_[4], W_[5]], writes=[W_[9]])
                pg.op("pe", lambda e, yb_=yb_, sl=sl, b=b, jj=jj: e.matmul(PS[yb_], lhsT=CTr[:, sl], rhs=hr[b][:], start=(jj == 0), stop=False), reads=[matB, W_[8]], writes=[PB[yb_]])
                pg.op("pe", lambda e, yb_=yb_, sl=sl, b=b, jj=jj: e.matmul(PS[yb_], lhsT=CTi[:, sl], rhs=hi[b][:], start=False, stop=(jj == 3)), reads=[matB, W_[9]], writes=[PB[yb_]])

            def tail(u):
                v = uvars(u)
                c, k, jj, j, b, px, pi_, W_, sl, tsl, usl, yb_ = (v[x] for x in ("c", "k", "jj", "j", "b", "px", "pi_", "W_", "sl", "tsl", "usl", "yb_"))
                pg.op("dve", lambda e, yb_=yb_, usl=usl, c=c: e.scalar_tensor_tensor(out=ysb[:], in0=UT[:, usl], scalar=s5d[:, c:c + 1], in1=PS[yb_][:], op0=ALU.mult, op1=ALU.add),
                      reads=[PB[yb_], uB, pcB], writes=[yB])
                pg.op("pool", lambda e: e.tensor_tensor(out=y2[:], in0=ysb[:], in1=ysb[:], op=ALU.mult), reads=[yB], writes=[yB])
                pg.op("pool", lambda e: e.tensor_scalar(out=y2[:], in0=y2[:], scalar1=0.044715, scalar2=1.0, op0=ALU.mult, op1=ALU.add), reads=[yB], writes=[yB])
                pg.op("pool", lambda e: e.tensor_tensor(out=y2[:], in0=y2[:], in1=ysb[:], op=ALU.mult), reads=[yB], writes=[yB])
                pg.op("act", lambda e: e.activation(out=yth[:], in_=y2[:], func=AF.Tanh, scale=0.7978845608028654), reads=[yB], writes=[yB])
                pg.op("pool", lambda e: e.tensor_scalar(out=ysb[:], in0=ysb[:], scalar1=0.5, scalar2=0.0, op0=ALU.mult, op1=ALU.add), reads=[yB], writes=[yB])
                stg, stgB, stgC = zring.nxt()
                pg.op("dve", lambda e, stg=stg: e.scalar_tensor_tensor(out=stg[:], in0=yth[:], scalar=1.0, in1=ysb[:], op0=ALU.add, op1=ALU.mult), reads=[yB], writes=[stgB])
                pg.dma("sp", stgC, lambda e, stg=stg, c=c, k=k: e.dma_start(out=s2p[0].ap()[c * 128:(c + 1) * 128, k * TB:(k + 1) * TB], in_=stg[:]),
                       reads=[stgB], writes=[B_s2[ns2[0] % 64]])
                ns2[0] += 1

            stage1(0)
            for u in range(len(units)):
                if u + 1 < len(units):
                    stage1(u + 1)
                stage2(u)
                if units[u][2] == 3:
                    tail(u)
            for i in range(2):
                pg.cc(cc2[i], lambda e, i=i: e.collective_compute("AllGather", ALU.bypass, replica_groups=PAIRS, ins=[s2p[i].ap()], outs=[o2p[i].ap()]),
                      reads=B_s2, writes=[B_o2])
            pg.barrier()
        if stop_after == "mixB":
            dC = pg.dma_ctr("dbg")
            pg.dma("sp", dC, lambda e: e.dma_start(out=dbg_d.ap()[0:512, :], in_=o2p[0].ap()), reads=[B_o2])
            pg.dma("sp", dC, lambda e: e.dma_start(out=dbg_d.ap()[512:1024, :], in_=o2p[1].ap()), reads=[B_o2])
            pg.streams["sp"].append(("wait", dC.sem, 32))
            return finish()

        with ExitStack() as st:
            mC = pg.dma_ctr("mixin")
            for slot in range(2):
                for kind in range(2):
                    for ci in range(2):
                        row0 = slot * 256 + ci * 128
                        hc = kind * 4 + slot * 2 + ci
                        pg.dma("sp", mC, lambda e, row0=row0, hc=hc, kind=kind: e.dma_start(out=hT[:, hc * NTOK:(hc + 1) * NTOK],
                                                                               in_=o2p[kind].ap()[row0:row0 + 128, bass.ds(PART(e, 0) * NTOK, NTOK)]),
                               reads=[B_o2], writes=hB[hc])
            wgl = sb("wgl", 4 * 512, BF16, st)
            wot = sb("wot", 8 * D, BF16, st)
            wcB = Buf("wc")
            wcC = pg.dma_ctr("wc")
            load_w(wgl, wcB, wcC, lambda k: wglu_d.ap()[k * 128:(k + 1) * 128, :], 4, 512, 512)
            load_w(wot, wcB, wcC, lambda k: wout_d.ap()[k * 128:(k + 1) * 128, :], 8, D, D)
            sg = [sb(f"sg{i}", TB, F32, st) for i in range(2)]
            sgB = [Buf() for _ in range(2)]
            yab = sb("yab", 4 * TB, BF16, st)
            yaB = [Buf() for _ in range(4)]
            n_ = 0
            for t in range(NTB):
                for mc in range(4):
                    pb = n_ % 2
                    s_ = n_ % 2
                    n_ += 1
                    for kc in range(4):
                        pg.op("pe", lambda e, pb=pb, kc=kc, mc=mc, t=t: e.matmul(PS[pb][:], lhsT=wgl[:, kc * 512 + mc * 128:kc * 512 + (mc + 1) * 128], rhs=HS(kc, t), start=(kc == 0), stop=(kc == 3)),
                              reads=[wcB, hB[kc][t]], writes=[PB[pb]], inc=(kc == 3))
                    pg.op("act", lambda e, pb=pb, s_=s_: e.activation(out=sg[s_][:], in_=PS[pb][:], func=AF.Tanh, scale=0.5), reads=[PB[pb]], writes=[sgB[s_]])
                    pg.op("dve", lambda e, s_=s_: e.tensor_scalar(out=sg[s_][:], in0=sg[s_][:], scalar1=0.5, scalar2=0.5, op0=ALU.mult, op1=ALU.add), reads=[], writes=[sgB[s_]])
                    pg.op("dve", lambda e, s_=s_, mc=mc, t=t: e.tensor_tensor(out=yab[:, mc * TB:(mc + 1) * TB], in0=sg[s_][:], in1=HS(mc, t), op=ALU.mult), reads=[sgB[s_], hB[mc][t]], writes=[yaB[mc]])
                for dc in range(8):
                    po = 2 + (dc % 2)
                    for i in range(8):
                        if i < 4:
                            pg.op("pe", lambda e, po=po, i=i, dc=dc: e.matmul(PS[po][:], lhsT=wot[:, i * D + dc * 128:i * D + (dc + 1) * 128], rhs=yab[:, i * TB:(i + 1) * TB], start=(i == 0), stop=False),
                                  reads=[wcB, yaB[i]], writes=[PB[po]], inc=False)
                        else:
                            pg.op("pe", lambda e, po=po, i=i, dc=dc, t=t: e.matmul(PS[po][:], lhsT=wot[:, i * D + dc * 128:i * D + (dc + 1) * 128], rhs=HS(i, t), start=False, stop=(i == 7)),
                                  reads=[wcB, hB[i][t]], writes=[PB[po]], inc=(i == 7))
                    pg.op("dve", lambda e, dc=dc, t=t, po=po: e.scalar_tensor_tensor(out=XS(dc, t), in0=PS[po][:], scalar=1.0, in1=XS(dc, t), op0=ALU.mult, op1=ALU.add),
                          reads=[PB[po]], writes=[xB[dc][t]])
            pg.barrier()
        dump_x("x2")
        if stop_after == "mixC":
            return finish()

        run_ffn(1, 2)
        dump_x("x3")
        if stop_after == "ffn2_0":
            return finish()
        run_ffn(2, 3)
        dump_x("x4")
        if stop_after == "ffn1_1":
            return finish()

        B_s3, B_o3 = Buf("s3"), Buf("o3")
        cc3 = pg.dma_ctr("cc3")
        with ExitStack() as st0:
            rmsnorm_to_hT(4, st0)
            pg.barrier()
        with ExitStack() as st:
            CW = NTOK + 2
            W1 = sb("scW1", 8 * D, BF16, st)
            W2 = sb("scW2", 8 * D, BF16, st)
            wso = sb("scWo", 8 * D, BF16, st)
            scw = sb("scw", 24, F32, st)
            w1B, w2B, woB = Buf("w1"), Buf("w2"), Buf("wo")
            w1C, w2C, woC = pg.dma_ctr("w1"), pg.dma_ctr("w2"), pg.dma_ctr("wo")
            pg.dma("sp", woC, lambda e: e.dma_start(out=scw[:], in_=scw_d.ap()), writes=[woB])
            load_w(W1, w1B, w1C, lambda k: scin_d.ap()[k * 128:(k + 1) * 128, D:2 * D], 8, D, D)
            load_w(W2, w2B, w2C, lambda k: scin_d.ap()[k * 128:(k + 1) * 128, 2 * D:3 * D], 8, D, D)
            load_w(wso, woB, woC, lambda k: scout_d.ap()[k * 128:(k + 1) * 128, :], 8, D, D)
            cvT = sb("cvT", 8 * CW, BF16, st)
            cvB = [[Buf() for _ in range(NTB)] for _ in range(8)]
            hlB = Buf("halo")
            csb = [sb(f"csb{i}", TB, F32, st) for i in range(2)]
            csB = [Buf() for _ in range(2)]
            n_ = 0
            for t in range(NTB):
                for fc in range(8):
                    pc_, pv_ = (n_ % 2), 2 + (n_ % 2)
                    s_ = n_ % 2
                    n_ += 1
                    for k in range(8):
                        pg.op("pe", lambda e, k=k, fc=fc, t=t, pc_=pc_: e.matmul(PS[pc_][:], lhsT=W1[:, k * D + fc * 128:k * D + (fc + 1) * 128], rhs=HS(k, t), start=(k == 0), stop=(k == 7)),
                              reads=[w1B, hB[k][t]], writes=[PB[pc_]], inc=(k == 7))
                    for k in range(8):
                        pg.op("pe", lambda e, k=k, fc=fc, t=t, pv_=pv_: e.matmul(PS[pv_][:], lhsT=W2[:, k * D + fc * 128:k * D + (fc + 1) * 128], rhs=HS(k, t), start=(k == 0), stop=(k == 7)),
                              reads=[w2B, hB[k][t]], writes=[PB[pv_]], inc=(k == 7))
                    pg.op("act", lambda e, pc_=pc_, s_=s_: e.activation(out=csb[s_][:], in_=PS[pc_][:], func=AF.Copy), reads=[PB[pc_]], writes=[csB[s_]])
                    pg.op("dve", lambda e, pv_=pv_, s_=s_, fc=fc, t=t: e.tensor_tensor(out=cvT[:, fc * CW + 2 + t * TB:fc * CW + 2 + (t + 1) * TB], in0=PS[pv_][:], in1=csb[s_][:], op=ALU.mult),
                          reads=[PB[pv_], csB[s_]], writes=[cvB[fc][t]])
            s3C = pg.dma_ctr("s3")
            for fc in range(8):
                pg.dma("sp", s3C, lambda e, fc=fc: e.dma_start(out=s3.ap()[:, fc * 2:fc * 2 + 2], in_=cvT[:, fc * CW + NTOK:fc * CW + NTOK + 2]),
                       reads=[cvB[fc][NTB - 1]], writes=[B_s3])
            pg.cc(cc3, lambda e: e.collective_compute("AllGather", ALU.bypass, replica_groups=PAIRS, ins=[s3.ap()], outs=[o3.ap()]),
                  reads=[B_s3], writes=[B_o3])
            halo_t = sb("halo_t", 16, BF16, st)
            hlC = pg.dma_ctr("hl")
            pg.dma("sp", hlC, lambda e: e.dma_start(out=halo_t[:], in_=o3.ap()[0:128, :]), reads=[B_o3], writes=[hlB])
            for fc in range(8):
                pg.op("dve", lambda e, fc=fc: e.tensor_scalar(out=cvT[:, fc * CW:fc * CW + 2], in0=halo_t[:, fc * 2:fc * 2 + 2], scalar1=flag[:, 0:1], scalar2=0.0, op0=ALU.mult, op1=ALU.add),
                      reads=[hlB, B_const], writes=[cvB[fc][0]])
            load_w(W1, w1B, w1C, lambda k: scin_d.ap()[k * 128:(k + 1) * 128, 0:D], 8, D, D)
            cy = csb
            cyB = csB
            gm = sb("gm", 8 * TB, BF16, st)
            gmB = [Buf() for _ in range(8)]
            n_ = 0
            for t in (1, 2, 3, 0):
                for fc in range(8):
                    pb = n_ % 2
                    s_ = n_ % 2
                    n_ += 1
                    for k in range(8):
                        pg.op("pe", lambda e, k=k, fc=fc, t=t, pb=pb: e.matmul(PS[pb][:], lhsT=W1[:, k * D + fc * 128:k * D + (fc + 1) * 128], rhs=HS(k, t), start=(k == 0), stop=(k == 7)),
                              reads=[w1B, hB[k][t]], writes=[PB[pb]], inc=(k == 7))
                    base = fc * CW + 2 + t * TB
                    rd = [cvB[fc][t]] + ([cvB[fc][t - 1]] if t > 0 else [])
                    pg.op("pool", lambda e, s_=s_, base=base, fc=fc: e.tensor_scalar(out=cy[s_][:], in0=cvT[:, base - 2:base - 2 + TB], scalar1=scw[:, 0 * 8 + fc:0 * 8 + fc + 1], scalar2=0.0, op0=ALU.mult, op1=ALU.add),
                          reads=rd + [woB], writes=[cyB[s_]])
                    pg.op("dve", lambda e, s_=s_, base=base, fc=fc: e.scalar_tensor_tensor(out=cy[s_][:], in0=cvT[:, base - 1:base - 1 + TB], scalar=scw[:, 1 * 8 + fc:1 * 8 + fc + 1], in1=cy[s_][:], op0=ALU.mult, op1=ALU.add),
                          reads=rd + [woB], writes=[cyB[s_]])
                    pg.op("dve", lambda e, s_=s_, base=base, fc=fc: e.scalar_tensor_tensor(out=cy[s_][:], in0=cvT[:, base:base + TB], scalar=scw[:, 2 * 8 + fc:2 * 8 + fc + 1], in1=cy[s_][:], op0=ALU.mult, op1=ALU.add),
                          reads=rd + [woB], writes=[cyB[s_]])
                    pg.op("dve", lambda e, s_=s_, pb=pb, fc=fc: e.tensor_tensor(out=gm[:, fc * TB:(fc + 1) * TB], in0=PS[pb][:], in1=cy[s_][:], op=ALU.mult),
                          reads=[PB[pb], cyB[s_]], writes=[gmB[fc]])
                for dc in range(8):
                    po = 2 + (dc % 2)
                    for fc in range(8):
                        pg.op("pe", lambda e, po=po, fc=fc, dc=dc: e.matmul(PS[po][:], lhsT=wso[:, fc * D + dc * 128:fc * D + (dc + 1) * 128], rhs=gm[:, fc * TB:(fc + 1) * TB], start=(fc == 0), stop=(fc == 7)),
                              reads=[woB, gmB[fc]], writes=[PB[po]], inc=(fc == 7))
                    pg.op("dve", lambda e, dc=dc, t=t, po=po: e.scalar_tensor_tensor(out=XS(dc, t), in0=PS[po][:], scalar=1.0, in1=XS(dc, t), op0=ALU.mult, op1=ALU.add),
                          reads=[PB[po]], writes=[xB[dc][t]])
            pg.barrier()
        dump_x("x5")
        if stop_after == "mix1":
            return finish()
        run_ffn(3, 5)
        dump_x("x6")
        if stop_after == "ffn2_1":
            return finish()

        with ExitStack() as st:
            ni = 6
            sq = sb("sqF", 8 * TB, BF16, st)
            rstd = sb("rstdF", TB, F32, st)
            sqB = [Buf() for _ in range(8)]
            rB = Buf()
            oring = Ring("outstg", 3, TB, F32, st)
            d_outs = []
            for t in range(NTB):
                for c in range(8):
                    if c % 2 == 0:
                        pg.op("act", lambda e, c=c, t=t: e.activation(out=sq[:, c * TB:(c + 1) * TB], in_=XS(c, t), func=AF.Square), reads=[xB[c][t]], writes=[sqB[c]])
                    else:
                        pg.op("pool", lambda e, c=c, t=t: e.tensor_tensor(out=sq[:, c * TB:(c + 1) * TB], in0=XS(c, t), in1=XS(c, t), op=ALU.mult), reads=[xB[c][t]], writes=[sqB[c]])
                for c in range(8):
                    pg.op("pe", lambda e, c=c: e.matmul(PS[7][:], lhsT=ones_bf[:], rhs=sq[:, c * TB:(c + 1) * TB], start=(c == 0), stop=(c == 7)), reads=[sqB[c], B_const], writes=[PB[7]], inc=(c == 7))
                pg.op("act", lambda e: e.activation(out=rstd[:], in_=PS[7][:], func=AF.Ln, bias=EPS), reads=[PB[7]], writes=[rB])
                pg.op("act", lambda e: e.activation(out=rstd[:], in_=rstd[:], func=AF.Exp, scale=-0.5), reads=[rB], writes=[rB])
                for c in range(8):
                    stg, stgB, stgC = oring.nxt()
                    pg.op("dve", lambda e, c=c, t=t, stg=stg: e.scalar_tensor_tensor(out=stg[:], in0=XS(c, t), scalar=norms[:, ni * 8 + c:ni * 8 + c + 1], in1=rstd[:], op0=ALU.mult, op1=ALU.mult),
                          reads=[xB[c][t], rB, B_const], writes=[stgB])
                    pg.dma("sp", stgC, lambda e, c=c, t=t, stg=stg: e.dma_start(out=out_d.ap()[c * 128:(c + 1) * 128, t * TB:(t + 1) * TB], in_=stg[:]), reads=[stgB])
            for c_ in oring.c:
                pg.streams["sp"].append(("wait", c_.sem, c_.val))
            pg.emit()
            return nc


_CACHE = {}


def _consts():
    ident = np.eye(128, dtype=np.float32)
    j = np.arange(128)[:, None]
    s = np.arange(128)[None, :]
    tri = (j >= s).astype(np.float32)
    masks = []
    t = np.arange(512)[None, :]
    for o in range(4):
        masks.append((t > 128 * o + j).astype(np.float32))
    iota = np.broadcast_to(np.arange(1, 513, dtype=np.float32)[None, :], (128, 512))
    return np.ascontiguousarray(np.concatenate([ident, tri] + masks + [iota], axis=1))


def _prep_inputs(inp):
    x = np.asarray(inp["x"], np.float32).reshape(8, NTOK, D)
    norm_list = [inp["ffn1_norm"][0], inp["mix_norm"][0], inp["ffn2_norm"][0],
                 inp["ffn1_norm"][1], inp["mix_norm"][1], inp["ffn2_norm"][1], inp["final_norm"]]
    norms = np.stack([np.asarray(n, np.float32).reshape(8, 128).T for n in norm_list], axis=1).reshape(128, 56)
    cst = _consts()
    win = np.asarray(inp["ab_w_in"][0], np.float32)
    maps = []
    for c in range(8):
        r = c % 2
        m = {"xT": np.ascontiguousarray(x[c].T), "norms": np.ascontiguousarray(norms), "cst": cst,
             "flag": np.full((128, 1), float(r), np.float32)}
        order = [(0, inp["ffn1_w_gate"], inp["ffn1_w_up"], inp["ffn1_w_down"]), (0, inp["ffn2_w_gate"], inp["ffn2_w_up"], inp["ffn2_w_down"]),
                 (1, inp["ffn1_w_gate"], inp["ffn1_w_up"], inp["ffn1_w_down"]), (1, inp["ffn2_w_gate"], inp["ffn2_w_up"], inp["ffn2_w_down"])]
        for i, (l, g, u, d) in enumerate(order):
            m[f"wg{i}"] = np.asarray(g[l], np.float32)
            m[f"wu{i}"] = np.asarray(u[l], np.float32)
            m[f"wd{i}"] = np.asarray(d[l], np.float32)

        def cols(rr):
            return np.concatenate([win[:, 256 * rr:256 * rr + 256], win[:, 512 + 256 * rr:512 + 256 * rr + 256],
                                   win[:, 1024 + 256 * rr:1024 + 256 * rr + 256], win[:, 1536 + 256 * rr:1536 + 256 * rr + 256]], axis=1)
        m["win"] = np.ascontiguousarray(np.concatenate([cols(r), cols(1 - r)], axis=1))
        m["wglu"] = np.asarray(inp["s5_w_glu"][0], np.float32)
        m["wout"] = np.asarray(inp["ab_w_out"][0], np.float32)
        m["scin"] = np.asarray(inp["sc_w_in"][0], np.float32)
        m["scw"] = np.ascontiguousarray(np.stack([np.asarray(inp["sc_conv_w"][0][k], np.float32).reshape(8, 128).T for k in range(3)], axis=1).reshape(128, 24))
        m["scout"] = np.asarray(inp["sc_w_out"][0], np.float32)
        g0 = 16 * r
        lr = np.asarray(inp["s5_lambda_re"][0][g0:g0 + 16], np.float32)
        li = np.asarray(inp["s5_lambda_im"][0][g0:g0 + 16], np.float32)
        ld = np.broadcast_to(np.asarray(inp["s5_log_dt"][0][g0:g0 + 16], np.float32)[:, None], (16, 64))

        def pl(a):
            return a.reshape(8, 2, 64).transpose(1, 2, 0).reshape(128, 8)
        z = np.zeros((128, 8), np.float32)
        m["lam"] = np.ascontiguousarray(np.concatenate([pl(lr), pl(li), pl(ld), z, z], axis=1))
        m["bre"] = np.ascontiguousarray(np.asarray(inp["s5_b_re"][0][g0:g0 + 16], np.float32))
        m["bim"] = np.ascontiguousarray(np.asarray(inp["s5_b_im"][0][g0:g0 + 16], np.float32))
        m["cre"] = np.ascontiguousarray(np.asarray(inp["s5_c_re"][0][g0:g0 + 16], np.float32))
        m["cim"] = np.ascontiguousarray(np.asarray(inp["s5_c_im"][0][g0:g0 + 16], np.float32))
        m["s5d"] = np.ascontiguousarray(np.asarray(inp["s5_d"][0][256 * r:256 * r + 256], np.float32).reshape(2, 128).T)
        maps.append(m)
    return maps


def kernel(**inputs):
    stop = os.environ.get("MK_STOP", "final")
    key = stop + os.environ.get("MK_DBG", "")
    if key not in _CACHE:
        _CACHE[key] = build(stop_after=stop)
    nc = _CACHE[key]
    maps = _prep_inputs(inputs)
    res = run_bass_kernel_spmd(nc, maps, core_ids=list(range(8)))
    _CACHE["last_res"] = res
    outT = np.stack([np.asarray(res.results[c]["outT"]) for c in range(8)], axis=0)
    out = outT.transpose(0, 2, 1).reshape(4, SEQ, D)
    return np.ascontiguousarray(out.astype(np.float32))
```
